# Optimizing a Trainium2 kernel written in Bass

```python
import math
import jax
import jax.numpy as jnp
from jax import lax
import numpy as np

D_MODEL = 1024
BATCH = 16
SEQ = 4096
DEPTH = 4

PLE_DIM = 256
D_FF = 2816
NORM_EPS = 1e-6
NEG_INF = -1e30
NUM_BUCKETS = 32
MAX_DISTANCE = 128
A_HEADS = 4
A_HEAD_DIM = 64
A_QBLOCK = 128
B_HEADS = 8
B_GROUPS = 2
B_REP = B_HEADS // B_GROUPS
B_HEAD_DIM = 64
B_CMP_LEN = 32
B_CMP_STRIDE = 16
B_CMP_HIDDEN = 256
B_SEL_BLOCK = 64
B_SEL_TOPK = 16
B_WINDOW = 512
B_QBLOCK = 64
B_SEL_FORCE = 1e6
C_HEADS = 8
C_GROUPS = 2
C_REP = C_HEADS // C_GROUPS
C_HEAD_DIM = 64
C_WINDOW = 128
C_QBLOCK = 128

A_WIDTH = A_HEADS * 2 * A_HEAD_DIM
B_WIDTH = B_HEADS * B_HEAD_DIM
B_KV = B_GROUPS * B_HEAD_DIM
C_WIDTH = C_HEADS * C_HEAD_DIM
C_KV = C_GROUPS * C_HEAD_DIM
N_BIAS_HEADS = A_HEADS + B_HEADS + C_HEADS
IN_SPLITS = (A_WIDTH, A_WIDTH, A_WIDTH,
             B_WIDTH, B_KV, B_KV, B_KV, B_KV, B_KV, B_KV, 3 * B_HEADS,
             C_WIDTH, C_KV, C_KV,
             D_MODEL, D_MODEL, D_MODEL)
D_IN = 3 * A_WIDTH + B_WIDTH + 6 * B_KV + 3 * B_HEADS + C_WIDTH + 2 * C_KV + 3 * D_MODEL

kernel_name = "hybrid_diff_nsa_swa_macaron_trunk"


def rms_norm(x, g):
    xf = x.astype(jnp.float32)
    y = xf * lax.rsqrt(jnp.mean(xf * xf, axis=-1, keepdims=True) + NORM_EPS)
    return (y * g.astype(jnp.float32)).astype(x.dtype)


def swiglu(x, wi, wo):
    gate, up = jnp.split(x @ wi, 2, axis=-1)
    return (jax.nn.silu(gate) * up) @ wo


def split_cols(z, sizes):
    outs, start = [], 0
    for s in sizes:
        outs.append(z[..., start:start + s])
        start += s
    return outs


def rel_bucket(dist):
    n = jnp.maximum(dist, 0)
    max_exact = NUM_BUCKETS // 2
    nf = jnp.maximum(n, 1).astype(jnp.float32)
    large = max_exact + (jnp.log(nf / max_exact) / math.log(MAX_DISTANCE / max_exact)
                         * (NUM_BUCKETS - max_exact)).astype(jnp.int32)
    large = jnp.minimum(large, NUM_BUCKETS - 1)
    return jnp.where(n < max_exact, n, large)


def head_bias(table, dist):
    return jnp.moveaxis(table[rel_bucket(dist)].astype(jnp.float32), -1, 0)


def grouped_head_bias(table, dist, groups, rep):
    b = head_bias(table, dist)
    return b.reshape((groups, rep) + dist.shape)


def masked_softmax(s, mask):
    return jax.nn.softmax(jnp.where(mask, s, NEG_INF), axis=-1) * mask


def diff_attention(q, k, v, bias_table, lam, lam_init, subln_g):
    bsz, seq = q.shape[:2]
    nblk = seq // A_QBLOCK
    scale = A_HEAD_DIM ** -0.5
    qb = q.reshape(bsz, nblk, A_QBLOCK, A_HEADS, 2, A_HEAD_DIM).swapaxes(0, 1)
    kpos = jnp.arange(seq)

    def block(args):
        i, qi = args
        qpos = i * A_QBLOCK + jnp.arange(A_QBLOCK)
        dist = qpos[:, None] - kpos[None, :]
        s = jnp.einsum('bqhcd,bkhcd->bchqk', qi, k, preferred_element_type=jnp.float32) * scale
        s = s + head_bias(bias_table, dist)[None, None]
        pr = masked_softmax(s, dist >= 0)
        attn = pr[:, 0] - lam * pr[:, 1]
        return jnp.einsum('bhqk,bkhe->bqhe', attn.astype(v.dtype), v)

    o = lax.map(block, (jnp.arange(nblk), qb))
    o = o.swapaxes(0, 1).reshape(bsz, seq, A_HEADS, 2 * A_HEAD_DIM)
    o = rms_norm(o, subln_g) * (1.0 - lam_init)
    return o.reshape(bsz, seq, A_WIDTH)


def nsa_attention(q, kc, vc, ks, vs, kw, vw, gates, bias_table, cmp_pos, cmp_w1, cmp_w2):
    bsz, seq = q.shape[:2]
    scale = B_HEAD_DIM ** -0.5
    n_chunk = seq // B_CMP_STRIDE
    n_cmp = n_chunk - 1
    n_sel = seq // B_SEL_BLOCK
    top_k = min(B_SEL_TOPK, n_sel)
    nblk = seq // B_QBLOCK

    def compress(t, pos, w1, w2):
        c = t.reshape(bsz, n_chunk, B_CMP_STRIDE, B_GROUPS, B_HEAD_DIM)
        blk = jnp.concatenate([c[:, :-1], c[:, 1:]], axis=2) + pos[None, None, :, None, :]
        blk = blk.transpose(0, 1, 3, 2, 4).reshape(bsz, n_cmp, B_GROUPS, B_CMP_LEN * B_HEAD_DIM)
        return jax.nn.gelu(blk @ w1) @ w2

    k_cmp = compress(kc, cmp_pos[0], cmp_w1[0], cmp_w2[0])
    v_cmp = compress(vc, cmp_pos[1], cmp_w1[1], cmp_w2[1])
    cmp_start = jnp.arange(n_cmp) * B_CMP_STRIDE
    cmp_end = cmp_start + B_CMP_LEN - 1
    sel_start = jnp.arange(n_sel) * B_SEL_BLOCK
    overlap = ((cmp_start[:, None] < sel_start[None, :] + B_SEL_BLOCK)
               & (cmp_start[:, None] + B_CMP_LEN > sel_start[None, :])).astype(jnp.float32)
    ks_blk = ks.reshape(bsz, n_sel, B_SEL_BLOCK, B_GROUPS, B_HEAD_DIM).transpose(0, 3, 1, 2, 4)
    vs_blk = vs.reshape(bsz, n_sel, B_SEL_BLOCK, B_GROUPS, B_HEAD_DIM).transpose(0, 3, 1, 2, 4)
    kw_pad = jnp.pad(kw, ((0, 0), (B_WINDOW, 0), (0, 0), (0, 0)))
    vw_pad = jnp.pad(vw, ((0, 0), (B_WINDOW, 0), (0, 0), (0, 0)))
    tbg = bias_table.reshape(NUM_BUCKETS, B_GROUPS, B_REP).transpose(1, 0, 2)
    b_idx = jnp.arange(bsz)[:, None, None, None]
    g_idx = jnp.arange(B_GROUPS)[None, :, None, None]
    blk_id = jnp.arange(n_sel)
    qb = q.reshape(bsz, nblk, B_QBLOCK, B_GROUPS, B_REP, B_HEAD_DIM).swapaxes(0, 1)
    gb = gates.reshape(bsz, nblk, B_QBLOCK, B_GROUPS, B_REP, 3).swapaxes(0, 1)

    def block(args):
        i, qi, gi = args
        qpos = i * B_QBLOCK + jnp.arange(B_QBLOCK)
        dist_c = qpos[:, None] - cmp_end[None, :]
        s_c = jnp.einsum('bqgrd,bngd->bgrqn', qi, k_cmp, preferred_element_type=jnp.float32) * scale
        s_c = s_c + grouped_head_bias(bias_table, dist_c, B_GROUPS, B_REP)
        p_c = masked_softmax(s_c, dist_c >= 0)
        o_c = jnp.einsum('bgrqn,bngd->bqgrd', p_c.astype(v_cmp.dtype), v_cmp)
        imp = jnp.einsum('bgrqn,nm->bgqm', p_c, overlap)
        cur = qpos // B_SEL_BLOCK
        forced = (blk_id[None, :] == 0) | (blk_id[None, :] == cur[:, None]) | (blk_id[None, :] == cur[:, None] - 1)
        future = blk_id[None, :] > cur[:, None]
        imp = jnp.where(forced, B_SEL_FORCE, jnp.where(future, -B_SEL_FORCE, imp))
        _, idx = lax.top_k(imp, top_k)
        n_keys = top_k * B_SEL_BLOCK
        k_sel = ks_blk[b_idx, g_idx, idx].reshape(bsz, B_GROUPS, B_QBLOCK, n_keys, B_HEAD_DIM)
        v_sel = vs_blk[b_idx, g_idx, idx].reshape(bsz, B_GROUPS, B_QBLOCK, n_keys, B_HEAD_DIM)
        kpos_s = (idx[..., None] * B_SEL_BLOCK + jnp.arange(B_SEL_BLOCK)).reshape(bsz, B_GROUPS, B_QBLOCK, n_keys)
        dist_s = qpos[None, None, :, None] - kpos_s
        bias_s = tbg[g_idx, rel_bucket(dist_s)].astype(jnp.float32).transpose(0, 1, 4, 2, 3)
        s_s = jnp.einsum('bqgrd,bgqld->bgrql', qi, k_sel, preferred_element_type=jnp.float32) * scale + bias_s
        p_s = masked_softmax(s_s, (dist_s >= 0)[:, :, None])
        o_s = jnp.einsum('bgrql,bgqld->bqgrd', p_s.astype(v_sel.dtype), v_sel)
        start = i * B_QBLOCK
        k_win = lax.dynamic_slice_in_dim(kw_pad, start, B_WINDOW + B_QBLOCK, axis=1)
        v_win = lax.dynamic_slice_in_dim(vw_pad, start, B_WINDOW + B_QBLOCK, axis=1)
        kpos_w = start - B_WINDOW + jnp.arange(B_WINDOW + B_QBLOCK)
        dist_w = qpos[:, None] - kpos_w[None, :]
        mask_w = (dist_w >= 0) & (dist_w < B_WINDOW) & (kpos_w >= 0)[None, :]
        s_w = jnp.einsum('bqgrd,bkgd->bgrqk', qi, k_win, preferred_element_type=jnp.float32) * scale
        s_w = s_w + grouped_head_bias(bias_table, dist_w, B_GROUPS, B_REP)
        p_w = masked_softmax(s_w, mask_w)
        o_w = jnp.einsum('bgrqk,bkgd->bqgrd', p_w.astype(v_win.dtype), v_win)
        return gi[..., 0:1] * o_c + gi[..., 1:2] * o_s + gi[..., 2:3] * o_w

    o = lax.map(block, (jnp.arange(nblk), qb, gb))
    return o.swapaxes(0, 1).reshape(bsz, seq, B_WIDTH)


def swa_sink_attention(q, k, v, bias_table, sinks):
    bsz, seq = q.shape[:2]
    scale = C_HEAD_DIM ** -0.5
    nblk = seq // C_QBLOCK
    k_pad = jnp.pad(k, ((0, 0), (C_WINDOW, 0), (0, 0), (0, 0)))
    v_pad = jnp.pad(v, ((0, 0), (C_WINDOW, 0), (0, 0), (0, 0)))
    qb = q.reshape(bsz, nblk, C_QBLOCK, C_GROUPS, C_REP, C_HEAD_DIM).swapaxes(0, 1)
    sink = jnp.broadcast_to(sinks.astype(jnp.float32).reshape(C_GROUPS, C_REP, 1, 1),
                            (bsz, C_GROUPS, C_REP, C_QBLOCK, 1))

    def block(args):
        i, qi = args
        start = i * C_QBLOCK
        qpos = start + jnp.arange(C_QBLOCK)
        kb = lax.dynamic_slice_in_dim(k_pad, start, C_WINDOW + C_QBLOCK, axis=1)
        vb = lax.dynamic_slice_in_dim(v_pad, start, C_WINDOW + C_QBLOCK, axis=1)
        kpos = start - C_WINDOW + jnp.arange(C_WINDOW + C_QBLOCK)
        dist = qpos[:, None] - kpos[None, :]
        mask = (dist >= 0) & (dist < C_WINDOW) & (kpos >= 0)[None, :]
        s = jnp.einsum('bqgrd,bkgd->bgrqk', qi, kb, preferred_element_type=jnp.float32) * scale
        s = jnp.where(mask, s + grouped_head_bias(bias_table, dist, C_GROUPS, C_REP), NEG_INF)
        pr = jax.nn.softmax(jnp.concatenate([s, sink], axis=-1), axis=-1)[..., :-1]
        return jnp.einsum('bgrqk,bkgd->bqgrd', pr.astype(vb.dtype), vb)

    o = lax.map(block, (jnp.arange(nblk), qb))
    return o.swapaxes(0, 1).reshape(bsz, seq, C_WIDTH)


def setup_inputs(seed: int = 0) -> dict:
    key = jax.random.key(seed)
    ks = jax.random.split(key, 20)

    def nrm(k, shape, scale):
        return jax.random.normal(k, shape, jnp.float32) * scale

    return {
        'x': nrm(ks[0], (BATCH, SEQ, D_MODEL), 1.0),
        'p': nrm(ks[1], (DEPTH, BATCH, SEQ, PLE_DIM), 1.0),
        'norm_g': 1.0 + nrm(ks[2], (DEPTH, 4, D_MODEL), 0.05),
        'ffn1_wi': nrm(ks[3], (DEPTH, D_MODEL, 2 * D_FF), D_MODEL ** -0.5),
        'ffn1_wo': nrm(ks[4], (DEPTH, D_FF, D_MODEL), D_FF ** -0.5),
        'w_in': nrm(ks[5], (DEPTH, D_MODEL, D_IN), D_MODEL ** -0.5),
        'diff_lambda': nrm(ks[6], (DEPTH, 4, A_HEAD_DIM), 0.1),
        'diff_subln': 1.0 + nrm(ks[7], (DEPTH, 2 * A_HEAD_DIM), 0.05),
        'nsa_cmp_pos': nrm(ks[8], (DEPTH, 2, B_CMP_LEN, B_HEAD_DIM), 0.1),
        'nsa_cmp_w1': nrm(ks[9], (DEPTH, 2, B_CMP_LEN * B_HEAD_DIM, B_CMP_HIDDEN), (B_CMP_LEN * B_HEAD_DIM) ** -0.5),
        'nsa_cmp_w2': nrm(ks[10], (DEPTH, 2, B_CMP_HIDDEN, B_HEAD_DIM), B_CMP_HIDDEN ** -0.5),
        'swa_sinks': nrm(ks[11], (DEPTH, C_HEADS), 0.5),
        'w_branch': nrm(ks[12], (DEPTH, 3, A_WIDTH, D_MODEL), A_WIDTH ** -0.5),
        'w_out': nrm(ks[13], (DEPTH, D_MODEL, D_MODEL), D_MODEL ** -0.5),
        'ffn2_wi': nrm(ks[14], (DEPTH, D_MODEL, 2 * D_FF), D_MODEL ** -0.5),
        'ffn2_wo': nrm(ks[15], (DEPTH, D_FF, D_MODEL), D_FF ** -0.5),
        'w_ple': nrm(ks[16], (DEPTH, PLE_DIM, D_MODEL), PLE_DIM ** -0.5),
        'w_ple_gate': nrm(ks[17], (DEPTH, D_MODEL, D_MODEL), D_MODEL ** -0.5),
        'rel_bias': nrm(ks[18], (NUM_BUCKETS, N_BIAS_HEADS), 0.5),
        'final_norm': 1.0 + nrm(ks[19], (D_MODEL,), 0.05),
    }


def reference(x, p, norm_g, ffn1_wi, ffn1_wo, w_in, diff_lambda, diff_subln, nsa_cmp_pos,
              nsa_cmp_w1, nsa_cmp_w2, swa_sinks, w_branch, w_out, ffn2_wi, ffn2_wo,
              w_ple, w_ple_gate, rel_bias, final_norm):
    bsz, seq, _ = x.shape
    bias_a = rel_bias[:, :A_HEADS]
    bias_b = rel_bias[:, A_HEADS:A_HEADS + B_HEADS]
    bias_c = rel_bias[:, A_HEADS + B_HEADS:]
    h = x
    for i in range(DEPTH):
        h = h + 0.5 * swiglu(rms_norm(h, norm_g[i, 0]), ffn1_wi[i], ffn1_wo[i])
        n = rms_norm(h, norm_g[i, 1])
        (aq, ak, av, bq, bkc, bvc, bks, bvs, bkw, bvw, bgate,
         cq, ck, cv, ga, gb, gc) = split_cols(n @ w_in[i], IN_SPLITS)
        lam_init = 0.8 - 0.6 * math.exp(-0.3 * i)
        lp = diff_lambda[i].astype(jnp.float32)
        lam = jnp.exp(jnp.sum(lp[0] * lp[1])) - jnp.exp(jnp.sum(lp[2] * lp[3])) + lam_init
        ya = diff_attention(aq.reshape(bsz, seq, A_HEADS, 2, A_HEAD_DIM),
                            ak.reshape(bsz, seq, A_HEADS, 2, A_HEAD_DIM),
                            av.reshape(bsz, seq, A_HEADS, 2 * A_HEAD_DIM),
                            bias_a, lam, lam_init, diff_subln[i])
        kvb = lambda t: t.reshape(bsz, seq, B_GROUPS, B_HEAD_DIM)
        yb = nsa_attention(bq.reshape(bsz, seq, B_GROUPS, B_REP, B_HEAD_DIM),
                           kvb(bkc), kvb(bvc), kvb(bks), kvb(bvs), kvb(bkw), kvb(bvw),
                           jax.nn.sigmoid(bgate.reshape(bsz, seq, B_GROUPS, B_REP, 3)),
                           bias_b, nsa_cmp_pos[i], nsa_cmp_w1[i], nsa_cmp_w2[i])
        yc = swa_sink_attention(cq.reshape(bsz, seq, C_GROUPS, C_REP, C_HEAD_DIM),
                                ck.reshape(bsz, seq, C_GROUPS, C_HEAD_DIM),
                                cv.reshape(bsz, seq, C_GROUPS, C_HEAD_DIM),
                                bias_c, swa_sinks[i])
        merged = (jax.nn.sigmoid(ga) * (ya @ w_branch[i, 0])
                  + jax.nn.sigmoid(gb) * (yb @ w_branch[i, 1])
                  + jax.nn.sigmoid(gc) * (yc @ w_branch[i, 2]))
        h = h + merged @ w_out[i]
        h = h + 0.5 * swiglu(rms_norm(h, norm_g[i, 2]), ffn2_wi[i], ffn2_wo[i])
        h = h + jax.nn.sigmoid(rms_norm(h, norm_g[i, 3]) @ w_ple_gate[i]) * (p[i] @ w_ple[i])
    return rms_norm(h, final_norm)
```

```python
import contextlib
import math
import os
import numpy as np
import concourse.bass as bass
import concourse.mybir as mybir
from concourse.bass_utils import run_bass_kernel_spmd

F32 = mybir.dt.float32
BF16 = mybir.dt.bfloat16
AF = mybir.ActivationFunctionType
ALU = mybir.AluOpType
AX = mybir.AxisListType

D = 1024
S = 4096
FF = 2816
PLE = 256
DIN = 6680
TC = 512
NCH = S // TC
NEG = -30000.0
EPS = 1e-6
NDMA = 8
CLEAR_ENG = os.environ.get('CLEAR_ENG', 'pool')
EPOCH = int(os.environ.get('EPOCH', 7000))
DEP = int(os.environ.get('DEP', 400))
SLOT = 256

C_AQ, C_AK, C_AV, C_BQ, C_BKC, C_BVC, C_BKS, C_BVS, C_BKW, C_BVW, C_BG = 0, 512, 1024, 1536, 2048, 2176, 2304, 2432, 2560, 2688, 2816
C_CQ, C_CK, C_CV, C_GA, C_GB, C_GC = 2840, 3352, 3480, 3608, 4632, 5656


class Res:
    __slots__ = ("w", "rs", "multi")

    def __init__(self, multi=False):
        self.w = {}
        self.rs = {}
        self.multi = multi


class T:
    __slots__ = ("ap", "res")

    def __init__(self, ap, res):
        self.ap = ap
        self.res = res

    def __getitem__(self, key):
        return T(self.ap[key], self.res)


class Sched:
    ENGS = ("pe", "act", "dve", "pool", "sp")

    def __init__(self):
        self.q = {e: [] for e in self.ENGS}
        self.cnt = {}
        self.bdone = {}
        self.known = {e: {} for e in self.ENGS}
        self.pending = {e: {} for e in self.ENGS}
        self.awaiting = {e: [] for e in self.ENGS}
        self.first_after = {}
        self.last_tok = {e: None for e in self.ENGS}
        self.dnext = {e: 0 for e in self.ENGS}
        self.nwaits = 0
        self.clear_tok = {}

    @staticmethod
    def _ep(stream):
        return EPOCH if len(stream) == 1 else DEP

    def _boundary(self, stream):
        n = self.cnt.get(stream, 0)
        EP = self._ep(stream)
        if n == 0 or n % EP != 0 or self.bdone.get(stream, 0) >= n // EP:
            return
        k = n // EP
        self.bdone[stream] = k
        for X in self.ENGS:
            self.pending[X][stream] = n
        if k >= 2:
            proofs = []
            fa = self.first_after.pop((stream, k - 1))
            for X in self.ENGS:
                p = fa.get(X) or self.last_tok[X]
                if p is not None:
                    proofs.append(p)
            self.clear_tok[(stream, k)] = self._append(CLEAR_ENG, ("clear", stream, (k - 2) % 3), proofs)
        self.first_after[(stream, k)] = {}
        ct = self.clear_tok.pop((stream, k - 1), None)
        for X in self.ENGS:
            if ct is not None and self.pending[X].get(ct[0], 0) < ct[1]:
                self.pending[X][ct[0]] = ct[1]
            self.awaiting[X].append((stream, k))

    def _append(self, eng, fn, needs, dma=False, notoken=False):
        if dma:
            slot = self.dnext[eng] % NDMA
            self.dnext[eng] += 1
            stream = (eng, slot)
        else:
            stream = (eng,)
        if not notoken:
            self._boundary(stream)
        waits = {}
        for tok in needs:
            if tok is None:
                continue
            st, c = tok
            if eng == "pe" and st == ("pe",):
                continue
            key = (st, (c - 1) // self._ep(st))
            if waits.get(key, 0) < c:
                waits[key] = c
        for st, c in self.pending[eng].items():
            key = (st, (c - 1) // self._ep(st))
            if waits.get(key, 0) < c:
                waits[key] = c
        self.pending[eng] = {}
        if dma:
            prev = self.cnt.get(stream, 0)
            if prev > 0:
                key = (stream, (prev - 1) // DEP)
                if waits.get(key, 0) < prev:
                    waits[key] = prev
        kn = self.known[eng]
        wl = []
        for key in sorted(waits.keys()):
            st = key[0]
            c = waits[key]
            if kn.get(st, 0) >= c:
                continue
            EP = self._ep(st)
            assert (c - 1) // EP >= self.bdone.get(st, 0) - 1, ("stale token", eng, st, c, self.cnt.get(st, 0))
            kn[st] = c
            wl.append((st, c))
        self.nwaits += len(wl)
        if notoken:
            self.q[eng].append((wl, fn, None))
            return None
        n = self.cnt.get(stream, 0)
        self.cnt[stream] = n + 1
        tok = (stream, n + 1)
        for key in self.awaiting[eng]:
            fa = self.first_after.get(key)
            if fa is not None:
                fa[eng] = tok
        self.awaiting[eng] = []
        self.last_tok[eng] = tok
        self.q[eng].append((wl, fn, tok))
        return tok

    def op(self, eng, fn, reads=(), writes=(), dma=False, is_out=False):
        needs = []
        for r in reads:
            needs.extend(r.w.items())
        for r in writes:
            if not (r.multi and dma):
                needs.extend(r.w.items())
            needs.extend(r.rs.items())
        tok = self._append(eng, fn, needs, dma=dma)
        st, c = tok
        for r in reads:
            if r.rs.get(st, 0) < c:
                r.rs[st] = c
        for r in writes:
            if r.multi and dma and not r.rs:
                if r.w.get(st, 0) < c:
                    r.w[st] = c
            else:
                r.w = {st: c}
                r.rs = {}
        return tok

    def emit(self, nc, es):
        last = [(st, c) for st, c in self.cnt.items() if len(st) == 2]
        self._append("sp", ("final",), last, notoken=True)
        sems = {}
        for st in sorted(self.cnt.keys()):
            for i in range(min(3, (self.cnt[st] - 1) // self._ep(st) + 1)):
                sems[(st, i)] = es.enter_context(nc.semaphore("s_" + "_".join(str(x) for x in st) + "_%d" % i))
        print("semaphores:", len(sems), "waits:", self.nwaits, flush=True)

        def semval(tok):
            st, c = tok
            EP = self._ep(st)
            ep = (c - 1) // EP
            return sems[(st, ep % 3)], (c - ep * EP) * (16 if len(st) == 2 else 1)

        block = es.enter_context(nc.Block())

        def run(e, eng):
            for wl, fn, tok in self.q[e]:
                if isinstance(fn, tuple):
                    for t in wl:
                        sm, v = semval(t)
                        eng.wait_ge(sm, v)
                    if fn[0] == "clear":
                        eng.sem_clear(sems[(fn[1], fn[2])])
                        own, _ = semval(tok)
                        eng.nop(nofuse=True).then_inc(own, 1)
                    continue
                for t in wl[:-1]:
                    sm, v = semval(t)
                    eng.wait_ge(sm, v)
                ins = fn(eng)
                if wl:
                    sm, v = semval(wl[-1])
                    ins._wait_ge(sm, v)
                own, _ = semval(tok)
                ins.then_inc(own, 16 if len(tok[0]) == 2 else 1)

        @block.tensor
        def _(eng):
            run("pe", eng)

        @block.scalar
        def _(eng):
            run("act", eng)

        @block.vector
        def _(eng):
            run("dve", eng)

        @block.gpsimd
        def _(eng):
            run("pool", eng)

        @block.sync
        def _(eng):
            run("sp", eng)


def _res_of(*ts):
    out = []
    for t in ts:
        if t is not None and not isinstance(t, (int, float)):
            out.extend(t.res)
    return out


def _ap(t):
    return t.ap if isinstance(t, T) else t


class K:
    def __init__(self, nc, es, arena_slots, n_static_f32):
        self.nc = nc
        self.es = es
        self.s = Sched()
        self.arena = es.enter_context(nc.sbuf_tensor("arena", [128, arena_slots * SLOT], F32))
        self.arena16 = self.arena[:, :].bitcast(BF16)
        self.ares = [Res() for _ in range(arena_slots)]
        self.nslots = arena_slots
        self.top = 0
        self.static = es.enter_context(nc.sbuf_tensor("static", [128, n_static_f32], F32))
        self.static16 = self.static[:, :].bitcast(BF16)
        self.stop_ = 0
        self.nstatic = n_static_f32
        self.banks = [T(es.enter_context(nc.psum_tensor("ps%d" % i, [128, 512], F32))[:, :], [Res()]) for i in range(8)]
        self.sring = 0
        self.dram_res = {}

    def mark(self):
        return self.top

    def release(self, m):
        self.top = m

    def alloc(self, n, dt=BF16, parts=128):
        nf = (n + 1) // 2 if dt == BF16 else n
        ns = (nf + SLOT - 1) // SLOT
        assert self.top + ns <= self.nslots, "arena overflow %d" % (self.top + ns)
        s0 = self.top
        self.top += ns
        res = self.ares[s0:s0 + ns]
        if dt == BF16:
            ap = self.arena16[0:parts, s0 * SLOT * 2: s0 * SLOT * 2 + n]
        else:
            ap = self.arena[0:parts, s0 * SLOT: s0 * SLOT + n]
        return T(ap, res)

    def salloc(self, n, dt=F32, parts=128):
        nf = (n + 1) // 2 if dt == BF16 else n
        nf = (nf + 7) // 8 * 8
        assert self.stop_ + nf <= self.nstatic, "static overflow"
        o = self.stop_
        self.stop_ += nf
        if dt == BF16:
            ap = self.static16[0:parts, o * 2: o * 2 + n]
        else:
            ap = self.static[0:parts, o: o + n]
        return T(ap, [Res()])

    def dres(self, key):
        r = self.dram_res.get(key)
        if r is None:
            r = self.dram_res[key] = Res(multi=True)
        return r

    def next_s(self):
        b = self.banks[self.sring % 4]
        self.sring += 1
        return b

    def dma(self, q, out, in_, is_out=False):
        o, i = _ap(out), _ap(in_)
        self.s.op(q, lambda e: e.dma_start(out=o, in_=i), reads=_res_of(in_), writes=_res_of(out), dma=True, is_out=is_out)

    def mm(self, out, lhsT, rhs, start=True, stop=True):
        o, l, r = _ap(out), _ap(lhsT), _ap(rhs)
        self.s.op("pe", lambda e: e.matmul(o, lhsT=l, rhs=r, start=start, stop=stop), reads=_res_of(lhsT, rhs), writes=_res_of(out))

    def act(self, out, in_, func, bias=None, scale=1.0, eng="act"):
        o, i = _ap(out), _ap(in_)
        kw = {}
        if bias is not None:
            kw["bias"] = _ap(bias)
        sc = _ap(scale)
        self.s.op(eng, lambda e: e.activation(out=o, in_=i, func=func, scale=sc, **kw),
                  reads=_res_of(in_, bias, scale), writes=_res_of(out))

    def ts(self, out, in0, s1, s2, op0, op1=None, eng="dve"):
        o, i = _ap(out), _ap(in0)
        a1, a2 = _ap(s1), _ap(s2)
        if op1 is None:
            f = lambda e: e.tensor_scalar(out=o, in0=i, scalar1=a1, scalar2=None, op0=op0)
        else:
            f = lambda e: e.tensor_scalar(out=o, in0=i, scalar1=a1, scalar2=a2, op0=op0, op1=op1)
        self.s.op(eng, f, reads=_res_of(in0, s1, s2), writes=_res_of(out))

    def tt(self, out, in0, in1, op, eng="dve"):
        o, a, b = _ap(out), _ap(in0), _ap(in1)
        self.s.op(eng, lambda e: e.tensor_tensor(out=o, in0=a, in1=b, op=op), reads=_res_of(in0, in1), writes=_res_of(out))

    def stt(self, out, in0, scalar, in1, op0, op1, eng="dve"):
        o, a, b, sc = _ap(out), _ap(in0), _ap(in1), _ap(scalar)
        self.s.op(eng, lambda e: e.scalar_tensor_tensor(out=o, in0=a, scalar=sc, in1=b, op0=op0, op1=op1),
                  reads=_res_of(in0, in1, scalar), writes=_res_of(out))

    def copy(self, out, in_, eng="dve"):
        o, i = _ap(out), _ap(in_)
        self.s.op(eng, lambda e: e.tensor_copy(out=o, in_=i), reads=_res_of(in_), writes=_res_of(out))

    def recip(self, out, in_):
        o, i = _ap(out), _ap(in_)
        self.s.op("dve", lambda e: e.reciprocal(out=o, in_=i), reads=_res_of(in_), writes=_res_of(out))

    def memset(self, out, val, eng="dve"):
        o = _ap(out)
        self.s.op(eng, lambda e: e.memset(o, val), writes=_res_of(out))

    def reduce_sum(self, out, in_):
        o, i = _ap(out), _ap(in_)
        self.s.op("dve", lambda e: e.reduce_sum(out=o, in_=i, axis=AX.X), reads=_res_of(in_), writes=_res_of(out))

    def max8(self, out, in_):
        o, i = _ap(out), _ap(in_)
        self.s.op("dve", lambda e: e.max(out=o, in_=i), reads=_res_of(in_), writes=_res_of(out))

    def match_replace(self, out, m8, vals, imm):
        o, m, v = _ap(out), _ap(m8), _ap(vals)
        self.s.op("dve", lambda e: e.match_replace(out=o, in_to_replace=m, in_values=v, imm_value=imm),
                  reads=_res_of(m8, vals), writes=_res_of(out))


def rel_bucket_np(dist):
    n = np.maximum(dist, 0)
    nf = np.maximum(n, 1).astype(np.float32)
    large = 16 + (np.log(nf / np.float32(16)) / np.float32(math.log(8.0)) * np.float32(16)).astype(np.int32)
    large = np.minimum(large, 31)
    return np.where(n < 16, n, large)


def host_tables(rel_bias):
    rb = np.asarray(rel_bias, np.float32)
    kk = np.arange(128)[:, None]
    def strip(ncols, heads, wmax):
        j = np.arange(ncols)[None, :]
        dist = j - kk - 384
        ok = (dist >= 0) if wmax is None else ((dist >= 0) & (dist < wmax))
        b = rel_bucket_np(dist)
        out = np.empty((len(heads), 128, ncols), np.float32)
        for i, h in enumerate(heads):
            out[i] = np.where(ok, rb[b, h], np.float32(NEG))
        return out
    st_a = strip(1280, range(0, 4), None)
    st_bs = strip(1280, range(4, 12), None)
    st_bw = strip(1408, range(4, 12), 512)
    st_c = strip(1024, range(12, 20), 128)
    cmp_t = np.full((8, 2, NCH, 128, 512), np.float32(NEG), np.float32)
    for nt in range(2):
        n = nt * 128 + np.arange(128)[:, None]
        for c in range(NCH):
            q = c * 512 + np.arange(512)[None, :]
            dist = q - (16 * n + 31)
            ok = (dist >= 0) & (n < 255)
            b = rel_bucket_np(dist)
            for h in range(8):
                cmp_t[h, nt, c] = np.where(ok, rb[b, 4 + h], np.float32(NEG))
    return st_a, st_bs, st_bw, st_c, cmp_t


def host_consts():
    ov = np.zeros((128, 2, 65), np.float32)
    for nt in range(2):
        for p in range(128):
            n = nt * 128 + p
            if n >= 255:
                continue
            ov[p, nt, 64] = 1.0
            for m in range(64):
                if (16 * n < 64 * m + 64) and (16 * n + 32 > 64 * m):
                    ov[p, nt, m] = 1.0
    expand = np.zeros((64, 32, 128), np.float32)
    for kt in range(32):
        for kq in range(128):
            expand[2 * kt + kq // 64, kt, kq] = 1.0
    keep = np.ones((128, 32, 64), np.float32)
    add = np.zeros((128, 32, 64), np.float32)
    for qt in range(32):
        for p in range(128):
            cur = (qt * 128 + p) // 64
            for m in range(64):
                forced = (m == 0) or (m == cur) or (m == cur - 1)
                if forced:
                    keep[p, qt, m] = 0.0
                    add[p, qt, m] = 1e6 + 64.0 * m
                elif m > cur:
                    keep[p, qt, m] = 0.0
                    add[p, qt, m] = -1e6 - 64.0 * m
    ident = np.eye(128, dtype=np.float32)
    return ov, expand, keep, add, ident


def build_program(NS, L, debug=False, phases=(1, 2, 3)):
    nc = bass.Bass("TRN2", target_bir_lowering=False)
    es = contextlib.ExitStack()
    with es:
        _build(nc, es, NS, L, debug, phases)
    return nc


def _build(nc, es, NS, L, debug, phases):
    def din(name, shape, dt=F32):
        return nc.dram_tensor(name, list(shape), dt, kind="ExternalInput").ap()

    def dscr(name, shape, dt=BF16, out=False):
        return nc.dram_tensor(name, list(shape), dt, kind=("ExternalOutput" if out else "Internal")).ap()

    k = K(nc, es, arena_slots=186, n_static_f32=2400)
    xT = din("xT", [NS, D, S])
    pT = din("pT", [L, NS, PLE, S])
    i_ng = din("ng", [128, L * 4 * 8])
    i_fn = din("fng", [128, 8])
    i_wi = [din("ffn1_wi", [L, D, 2 * FF]), din("ffn2_wi", [L, D, 2 * FF])]
    i_wo = [din("ffn1_wo", [L, FF, D]), din("ffn2_wo", [L, FF, D])]
    i_win = din("w_in", [L, D, DIN])
    i_dlam = din("dlam", [128, L * 4 * 64])
    i_subln = din("subln", [128, L])
    i_pos = din("posT", [128, L * 2 * 32])
    i_w1 = din("cmp_w1", [L, 2, 2048, 256])
    i_w2 = din("cmp_w2", [L, 2, 256, 64])
    i_sink = din("sinks", [128, L * 8])
    i_wbr = din("w_branch", [L, 3, 512, D])
    i_wout = din("w_out", [L, D, D])
    i_wple = din("w_ple", [L, PLE, D])
    i_wpg = din("w_ple_gate", [L, D, D])
    i_rb31 = din("rb31", [128, 20])
    i_sta = din("st_a", [4, 128, 1280])
    i_stbs = din("st_bs", [8, 128, 1280])
    i_stbw = din("st_bw", [8, 128, 1408])
    i_stc = din("st_c", [8, 128, 1024])
    i_cmpt = din("cmp_t", [8, 2, NCH, 128, 512])
    i_ov = din("ov", [128, 2, 65])
    i_expand = din("expand", [64, 32, 128])
    i_keep = din("keep", [128, 32, 64])
    i_add = din("addc", [128, 32, 64])
    i_ident = din("ident", [128, 128])
    yT = dscr("yT", [NS, D, S], F32, out=True)

    hT = dscr("hT", [NS, D, S], F32)
    b_wi = [dscr("b_wi%d" % i, [L, D, 2 * FF]) for i in range(2)]
    b_wo = [dscr("b_wo%d" % i, [L, FF, D]) for i in range(2)]
    b_win = dscr("b_win", [L, D, DIN])
    b_w1 = dscr("b_w1", [L, 2, 2048, 256])
    b_w2 = dscr("b_w2", [L, 2, 256, 64])
    b_wbr = dscr("b_wbr", [L, 3, 512, D])
    b_wout = dscr("b_wout", [L, D, D])
    b_wple = dscr("b_wple", [L, PLE, D])
    b_wpg = dscr("b_wpg", [L, D, D])
    b_sta = dscr("b_sta", [4, 128, 1280])
    b_stbs = dscr("b_stbs", [8, 128, 1280])
    b_stbw = dscr("b_stbw", [8, 128, 1408])
    b_stc = dscr("b_stc", [8, 128, 1024])
    b_cmpt = dscr("b_cmpt", [8, 2, NCH, 128, 512])
    NFB = 46
    zf = dscr("zf", [NFB, 128, S], BF16, out=debug)
    zav = dscr("zav", [S, 512], BF16, out=debug)
    zv3 = dscr("zv3", [3, S, 128], BF16, out=debug)
    yaT = dscr("yaT", [4, 128, S], BF16, out=debug)
    ybT = dscr("ybT", [3, 4, 128, S], BF16, out=debug)
    ycT = dscr("ycT", [4, 128, S], BF16, out=debug)
    dbg_h = dscr("dbg_h", [3, D, S], F32, out=True) if debug else None
    dbg_sel = dscr("dbg_sel", [2, 64, S], BF16, out=True) if debug else None
    dbg_imp = dscr("dbg_imp", [2, 128, 32 * 64], F32, out=True) if debug else None

    def DR(ap, *keys):
        return T(ap, [k.dres(x) for x in keys])

    ALLCH = list(range(NCH))

    ones16 = k.salloc(128, BF16)
    ident16 = k.salloc(128, BF16)
    ng_sb = k.salloc(L * 4 * 8, F32)
    fn_sb = k.salloc(8, F32)
    rb31 = k.salloc(20, F32)
    subln_sb = k.salloc(L, F32)
    sink_sb = k.salloc(L * 8, F32)
    esink = k.salloc(L * 8, F32)
    lamt = k.salloc(L * 4 * 64, F32)
    neglam = k.salloc(L, F32)
    ov16 = k.salloc(2 * 65, BF16)
    pos16 = k.salloc(L * 2 * 32, BF16)
    tiny = k.salloc(8, F32)

    k.memset(ones16, 1.0)
    k.memset(tiny, 0.0)
    k.dma("pool", ident16, DR(i_ident, "in"))
    k.dma("sp", ng_sb, DR(i_ng, "in"))
    k.dma("sp", fn_sb, DR(i_fn, "in"))
    k.dma("sp", rb31, DR(i_rb31, "in"))
    k.dma("sp", subln_sb, DR(i_subln, "in"))
    k.dma("sp", sink_sb, DR(i_sink, "in"))
    k.dma("sp", lamt, DR(i_dlam, "in"))
    k.dma("pool", ov16, DR(i_ov.rearrange("p a b -> p (a b)"), "in"))
    k.dma("pool", pos16, DR(i_pos, "in"))
    k.act(esink, sink_sb, AF.Exp)
    lam_inits = [0.8 - 0.6 * math.exp(-0.3 * i) for i in range(L)]
    m0 = k.mark()
    tmpl = k.alloc(64, F32)
    tmps = k.alloc(8, F32)
    for l in range(L):
        for j in range(2):
            a = lamt[:, (l * 4 + 2 * j) * 64:(l * 4 + 2 * j + 1) * 64]
            b = lamt[:, (l * 4 + 2 * j + 1) * 64:(l * 4 + 2 * j + 2) * 64]
            k.tt(tmpl, a, b, ALU.mult)
            k.reduce_sum(tmps[:, j:j + 1], tmpl)
        k.act(tmps[:, 2:4], tmps[:, 0:2], AF.Exp)
        k.stt(neglam[:, l:l + 1], tmps[:, 3:4], -lam_inits[l], tmps[:, 2:3], ALU.add, ALU.subtract)
    k.release(m0)

    for src, dst, n in ((i_sta, b_sta, 4), (i_stbs, b_stbs, 8), (i_stbw, b_stbw, 8), (i_stc, b_stc, 8)):
        for h in range(n):
            k.dma("pool", DR(dst[h], "tab"), DR(src[h], "in"))
    for h in range(8):
        for nt in range(2):
            k.dma("pool", DR(b_cmpt[h, nt].rearrange("c p q -> p c q"), "tab"), DR(i_cmpt[h, nt].rearrange("c p q -> p c q"), "in"))

    def cast_weights(l):
        wk = ("w", l)
        for i in range(2):
            for r in range(8):
                k.dma("pool", DR(b_wi[i][l, r * 128:(r + 1) * 128, :], wk), DR(i_wi[i][l, r * 128:(r + 1) * 128, :], "in"))
            for r in range(22):
                k.dma("pool", DR(b_wo[i][l, r * 128:(r + 1) * 128, :], wk), DR(i_wo[i][l, r * 128:(r + 1) * 128, :], "in"))
        for r in range(8):
            k.dma("pool", DR(b_win[l, r * 128:(r + 1) * 128, :], wk), DR(i_win[l, r * 128:(r + 1) * 128, :], "in"))
            k.dma("pool", DR(b_wout[l, r * 128:(r + 1) * 128, :], wk), DR(i_wout[l, r * 128:(r + 1) * 128, :], "in"))
            k.dma("pool", DR(b_wpg[l, r * 128:(r + 1) * 128, :], wk), DR(i_wpg[l, r * 128:(r + 1) * 128, :], "in"))
        for m in range(3):
            for r in range(4):
                k.dma("pool", DR(b_wbr[l, m, r * 128:(r + 1) * 128, :], wk), DR(i_wbr[l, m, r * 128:(r + 1) * 128, :], "in"))
        for r in range(2):
            k.dma("pool", DR(b_wple[l, r * 128:(r + 1) * 128, :], wk), DR(i_wple[l, r * 128:(r + 1) * 128, :], "in"))
        for kv in range(2):
            for r in range(16):
                k.dma("pool", DR(b_w1[l, kv, r * 128:(r + 1) * 128, :], wk), DR(i_w1[l, kv, r * 128:(r + 1) * 128, :], "in"))
            for r in range(2):
                k.dma("pool", DR(b_w2[l, kv, r * 128:(r + 1) * 128, :], wk), DR(i_w2[l, kv, r * 128:(r + 1) * 128, :], "in"))

    def rmsnorm(h8, gcol, nT, sq, rstd):
        for kc in range(8):
            k.act(sq[kc], h8[kc], AF.Square)
        ps = k.next_s()
        for kc in range(8):
            k.mm(ps, ones16, sq[kc], start=(kc == 0), stop=(kc == 7))
        k.act(rstd, ps, AF.Sqrt, bias=EPS, scale=1.0 / D)
        k.recip(rstd, rstd)
        for kc in range(8):
            k.stt(nT[kc], h8[kc], gcol(kc), rstd, ALU.mult, ALU.mult)

    def ffn(l, which, h8, nT, tl):
        wi = b_wi[which][l].rearrange("(kc p) c -> p kc c", p=128)
        wo = b_wo[which][l].rearrange("(j p) c -> p j c", p=128)
        wk = ("w", l)
        actT = tl["act"]
        for jb in range(11):
            wg = tl["wi"][(2 * jb) % 4]
            wu = tl["wi"][(2 * jb + 1) % 4]
            k.dma("sp", T(wg.ap.rearrange("p (a b) -> p a b", a=8), wg.res), DR(wi[:, :, jb * 256:(jb + 1) * 256], wk))
            k.dma("sp", T(wu.ap.rearrange("p (a b) -> p a b", a=8), wu.res), DR(wi[:, :, FF + jb * 256:FF + (jb + 1) * 256], wk))
            for jj in range(2):
                j = 2 * jb + jj
                pg = k.banks[4 + (2 * j) % 4]
                pu = k.banks[4 + (2 * j + 1) % 4]
                for kc in range(8):
                    k.mm(pg, wg[:, kc * 256 + jj * 128: kc * 256 + jj * 128 + 128], nT[kc], start=(kc == 0), stop=(kc == 7))
                for kc in range(8):
                    k.mm(pu, wu[:, kc * 256 + jj * 128: kc * 256 + jj * 128 + 128], nT[kc], start=(kc == 0), stop=(kc == 7))
                sg = tl["sg"][j % 2]
                k.act(sg, pg, AF.Silu)
                k.tt(actT[j], sg, pu, ALU.mult)
        for qd in range(4):
            wt = tl["wo"][qd % 2]
            k.dma("sp", T(wt.ap.rearrange("p (a b) -> p a b", a=22), wt.res), DR(wo[:, :, qd * 256:(qd + 1) * 256], wk))
            for mm_ in range(2):
                m = qd * 2 + mm_
                po = k.next_s()
                for j in range(22):
                    k.mm(po, wt[:, j * 256 + mm_ * 128: j * 256 + mm_ * 128 + 128], actT[j], start=(j == 0), stop=(j == 21))
                k.stt(h8[m], po, 0.5, h8[m], ALU.mult, ALU.add)

    def dense_tiles():
        tl = {}
        tl["h8"] = [[k.alloc(512, F32) for _ in range(8)] for _ in range(2)]
        tl["sq"] = [k.alloc(512, BF16) for _ in range(8)]
        tl["nT"] = [[k.alloc(512, BF16) for _ in range(8)] for _ in range(2)]
        tl["rstd"] = [k.alloc(512, F32) for _ in range(2)]
        tl["act"] = [k.alloc(512, BF16) for _ in range(22)]
        tl["wi"] = [k.alloc(8 * 256, BF16) for _ in range(4)]
        tl["wo"] = [k.alloc(22 * 256, BF16) for _ in range(2)]
        tl["sg"] = [k.alloc(512, F32) for _ in range(2)]
        return tl

    FB = []
    for h in range(4):
        FB.append((h, [(C_AQ + 128 * h, 128)], "q"))
    for h in range(4):
        FB.append((4 + h, [(C_AK + 128 * h, 128)], "c"))
    for b in range(4):
        FB.append((8 + b, [(C_BQ + 64 * b, 64), (C_BQ + 64 * (b + 4), 64)], "q"))
    FB.append((12, [(C_BKC, 128)], "c"))
    FB.append((13, [(C_BVC, 128)], "c"))
    FB.append((14, [(C_BKS, 128)], "c"))
    FB.append((15, [(C_BKW, 128)], "c"))
    FB.append((16, [(C_BG, 24)], "s"))
    for b in range(4):
        FB.append((17 + b, [(C_CQ + 64 * b, 64), (C_CQ + 64 * (b + 4), 64)], "q"))
    FB.append((21, [(C_CK, 128)], "c"))
    for i in range(8):
        FB.append((22 + i, [(C_GA + 128 * i, 128)], "s"))
    for i in range(8):
        FB.append((30 + i, [(C_GB + 128 * i, 128)], "s"))
    for i in range(8):
        FB.append((38 + i, [(C_GC + 128 * i, 128)], "s"))

    def phase1(l, s):
        m0 = k.mark()
        tl = dense_tiles()
        wt_in = [k.alloc(8 * 128, BF16) for _ in range(3)]
        wt_av = [k.alloc(8 * 512, BF16) for _ in range(1)]
        ev = [k.alloc(512, BF16) for _ in range(4)]
        win = b_win[l].rearrange("(kc p) c -> p kc c", p=128)
        wk = ("w", l)
        src = xT if l == 0 else hT
        import os
        STAGE = int(os.environ.get("STAGE", "9"))
        for c in range(int(os.environ.get("NCHT", NCH))):
            h8 = tl["h8"][c % 2]
            nT = tl["nT"][c % 2]
            rstd = tl["rstd"][c % 2]
            tok = slice(c * TC, (c + 1) * TC)
            hkey = ("h", s, c)
            for kc in range(8):
                k.dma("sp", h8[kc], DR(src[s, kc * 128:(kc + 1) * 128, tok], hkey if l > 0 else "in"))
            rmsnorm(h8, lambda kc: ng_sb[:, (l * 4 + 0) * 8 + kc:(l * 4 + 0) * 8 + kc + 1], nT, tl["sq"], rstd)
            if STAGE >= 2:
                ffn(l, 0, h8, nT, tl)
            for kc in range(8):
                k.dma("pool", DR(hT[s, kc * 128:(kc + 1) * 128, tok], hkey), h8[kc])
                if debug and l == 0 and s == 0:
                    k.dma("pool", DR(dbg_h[0, kc * 128:(kc + 1) * 128, tok], "dbg"), h8[kc], is_out=True)
            if STAGE < 3:
                continue
            nT2 = tl["nT"][(c + 1) % 2]
            rmsnorm(h8, lambda kc: ng_sb[:, (l * 4 + 1) * 8 + kc:(l * 4 + 1) * 8 + kc + 1], nT2, tl["sq"], rstd)
            zkey = ("z", c)
            for bi, (zb, cols, kind) in enumerate(FB):
                wt = wt_in[bi % 3]
                w3 = T(wt.ap.rearrange("p (a b) -> p a b", a=8), wt.res)
                off = 0
                for (c0, ncol) in cols:
                    k.dma("sp", w3[:, :, off:off + ncol], DR(win[:, :, c0:c0 + ncol], wk))
                    off += ncol
                M = off
                ps = k.next_s()
                for kc in range(8):
                    k.mm(ps[0:M, :], wt[:, kc * 128: kc * 128 + M], nT2[kc], start=(kc == 0), stop=(kc == 7))
                e = ev[bi % 4]
                if kind == "q":
                    k.act(e[0:M, :], ps[0:M, :], AF.Copy, scale=0.125) if bi % 2 == 0 else k.ts(e[0:M, :], ps[0:M, :], 0.125, None, ALU.mult)
                elif kind == "c":
                    k.act(e[0:M, :], ps[0:M, :], AF.Copy) if bi % 2 == 0 else k.copy(e[0:M, :], ps[0:M, :])
                else:
                    k.act(e[0:M, :], ps[0:M, :], AF.Sigmoid)
                k.dma("pool", DR(zf[zb, 0:M, tok], zkey), e[0:M, :])
            if STAGE < 4:
                continue
            wav = wt_av[0]
            k.dma("sp", T(wav.ap.rearrange("p (a b) -> p a b", a=8), wav.res), DR(win[:, :, C_AV:C_AV + 512], wk))
            for tt_ in range(4):
                ps = k.next_s()
                for kc in range(8):
                    k.mm(ps, nT2[kc][:, tt_ * 128:(tt_ + 1) * 128], wav[:, kc * 512:(kc + 1) * 512], start=(kc == 0), stop=(kc == 7))
                e = ev[tt_ % 4]
                k.copy(e, ps)
                k.dma("pool", DR(zav[c * TC + tt_ * 128: c * TC + (tt_ + 1) * 128, :], zkey), e)
            wv = wt_in[0]
            w3 = T(wv.ap.rearrange("p (a b) -> p a b", a=8), wv.res)
            for vi, c0 in enumerate((C_BVS, C_BVW, C_CV)):
                wv = wt_in[vi % 3]
                w3 = T(wv.ap.rearrange("p (a b) -> p a b", a=8), wv.res)
                k.dma("sp", w3, DR(win[:, :, c0:c0 + 128], wk))
                ps = k.next_s()
                for tt_ in range(4):
                    for kc in range(8):
                        k.mm(ps[:, tt_ * 128:(tt_ + 1) * 128], nT2[kc][:, tt_ * 128:(tt_ + 1) * 128], wv[:, kc * 128:(kc + 1) * 128],
                             start=(kc == 0), stop=(kc == 7))
                e = ev[vi % 4]
                k.act(e, ps, AF.Copy)
                k.dma("pool", DR(zv3[vi, c * TC:(c + 1) * TC, :].rearrange("(t p) e -> p t e", p=128), zkey),
                      T(e.ap.rearrange("p (t e) -> p t e", t=4), e.res))
        k.release(m0)

    def run_attn(steps, Pb):
        n = len(steps)
        deferred = []
        for i in range(n + 2):
            due = [d for d in deferred if d[0] <= i]
            deferred = [d for d in deferred if d[0] > i]
            for d in due:
                d[1]()
            if i < n:
                st = steps[i]
                Sb = k.next_s()
                nk = st["nk"]
                nq = len(st["qk"])
                for qi, (lh, rh) in enumerate(st["qk"]):
                    k.mm(Sb[0:nk, :], lh, rh, start=(qi == 0), stop=(qi == nq - 1))
                P = Pb[i % 4]
                if st["bias"] is not None and os.environ.get("A_NOBIAS", "0") == "0":
                    k.act(P[0:nk, :], Sb[0:nk, :], AF.Exp, bias=st["bias"][0:nk, :])
                else:
                    k.act(P[0:nk, :], Sb[0:nk, :], AF.Exp)
            j = i - 2
            if j >= 0:
                st = steps[j]
                P = Pb[j % 4]
                nk = st["nk"]
                for (o, lh, sta, sto) in st["pv"]:
                    k.mm(o, lh, P[0:nk, :], start=sta, stop=sto)
                if st.get("pv2"):
                    st["pv2"](P, Pb[(j - 1) % 4])
                if st.get("after"):
                    cont = st["after"]()
                    if cont is not None:
                        deferred.append((i + 2, cont))
        for d in deferred:
            d[1]()

    def phase2(l, s):
        m0 = k.mark()
        Pb = [k.alloc(512, BF16) for _ in range(4)]
        qT = [k.alloc(S, BF16) for _ in range(2)]
        kT = [k.alloc(S, BF16) for _ in range(2)]
        Vt = [k.alloc(32 * 128, BF16) for _ in range(2)]
        stp = [k.alloc(1280, BF16) for _ in range(2)]
        t0 = [k.alloc(512, F32) for _ in range(2)]
        rr = [k.alloc(512, F32) for _ in range(2)]
        osq = k.alloc(512, BF16)
        rs2 = k.alloc(512, F32)
        yo = [k.alloc(512, BF16) for _ in range(2)]
        zall = [("z", c) for c in ALLCH]
        job = 0
        import os
        P2 = os.environ.get("P2", "ACWMS")
        for h in (range(4) if "A" in P2 else []):
            q_, k_, v_, st_ = qT[h % 2], kT[h % 2], Vt[h % 2], stp[h % 2]
            k.dma("sp", q_, DR(zf[h], *zall))
            k.dma("sp", k_, DR(zf[4 + h], *zall))
            v3_ = T(v_.ap.rearrange("p (t e) -> p t e", t=32), v_.res)
            for t8 in range(4):
                k.dma("sp", v3_[:, t8 * 8:(t8 + 1) * 8, :],
                      DR(zav[t8 * 1024:(t8 + 1) * 1024, h * 128:(h + 1) * 128].rearrange("(t p) e -> p t e", p=128), *zall))
            k.dma("sp", st_, DR(b_sta[h], "tab"))
            steps = []
            for c in range(int(os.environ.get("A_NC", NCH))):
                for mp in range(2):
                    ob = k.banks[4 + (job % 2) * 2]
                    sb_ = k.banks[5 + (job % 2) * 2]
                    job += 1
                    kts = list(range(0, 4 * c + 4))
                    for kt in kts:
                        qk = [(k_[mp * 64:(mp + 1) * 64, kt * 128:(kt + 1) * 128], q_[mp * 64:(mp + 1) * 64, c * TC:(c + 1) * TC])]
                        bias = None
                        if kt >= 4 * c - 1:
                            off = (512 * c - 128 * kt) + 384
                            qk.append((ident16, st_[:, off:off + 512]))
                        else:
                            qk.append((ident16, st_[:, 768:1280]))
                        first, last = (kt == kts[0]), (kt == kts[-1])
                        stp_ = dict(qk=qk, nk=128, bias=bias,
                                    pv=[(ob, v_[:, kt * 128:(kt + 1) * 128], first, last), (sb_, ones16, first, last)])
                        if last and os.environ.get("A_AFTER", "1") == "1":
                            def after(ob=ob, sb_=sb_, mp=mp, c=c, h=h):
                                r = rr[mp]
                                k.recip(r, sb_)
                                if mp == 0:
                                    k.tt(t0[0], ob, r, ALU.mult)
                                else:
                                    k.tt(t0[1], ob, r, ALU.mult)
                                    k.stt(t0[0], t0[1], neglam[:, l:l + 1], t0[0], ALU.mult, ALU.add)
                                    k.act(osq, t0[0], AF.Square)

                                    def cont(c=c, h=h):
                                        ps = k.next_s()
                                        k.mm(ps, ones16, osq)
                                        k.act(rs2, ps, AF.Sqrt, bias=EPS, scale=1.0 / 128)
                                        k.recip(rs2, rs2)
                                        k.ts(rs2, rs2, 1.0 - lam_inits[l], None, ALU.mult)
                                        y = yo[c % 2]
                                        k.stt(y, t0[0], subln_sb[:, l:l + 1], rs2, ALU.mult, ALU.mult)
                                        k.dma("pool", DR(yaT[h, :, c * TC:(c + 1) * TC], ("ya", c)), y)
                                    return cont
                                return None
                            stp_["after"] = after
                        steps.append(stp_)
            run_attn(steps, Pb)
        k.release(m0)


        m0 = k.mark()
        Pb = [k.alloc(512, BF16) for _ in range(4)]
        qT = [k.alloc(S, BF16) for _ in range(2)]
        kTb = k.alloc(S, BF16)
        Va = k.alloc(32 * 2 * 65 + 64, BF16)
        Va4 = T(Va.ap[:, 0:32 * 2 * 65].rearrange("p (t g e) -> p t g e", t=32, g=2), Va.res)
        stp = [k.alloc(1408, BF16) for _ in range(2)]
        rrow = [k.alloc(512, F32) for _ in range(4)]
        bc1 = [k.alloc(512, F32, parts=64) for _ in range(4)]
        bc2 = [k.alloc(512, BF16, parts=64) for _ in range(4)]
        t1 = [k.alloc(512, F32) for _ in range(2)]
        yo = [k.alloc(512, BF16) for _ in range(4)]
        ones32 = k.alloc(64, F32)
        k.memset(ones32, 1.0)
        jobc = [0]

        def load_v(vi):
            k.memset(Va, 1.0)
            for g in range(2):
                for t8 in range(4):
                    k.dma("sp", Va4[:, t8 * 8:(t8 + 1) * 8, g, 0:64],
                          DR(zv3[vi, t8 * 1024:(t8 + 1) * 1024, g * 64:(g + 1) * 64].rearrange("(t p) e -> p t e", p=128), *zall))

        def finalize(accb, base, hh, c, dst, dkey, gate_row=None, sink=None, defer=False):
            i = jobc[0] % 4
            jobc[0] += 1
            r = rrow[i]
            if sink is not None:
                k.ts(r[64:65, :], accb[64:65, :], sink, None, ALU.add)
            else:
                k.ts(r[64:65, :], accb[64:65, :], 1e-30, None, ALU.max)
            k.recip(r[64:65, :], r[64:65, :])
            if gate_row is not None:
                k.dma("sp", bc2[i], DR(zf[16, gate_row:gate_row + 1, c * TC:(c + 1) * TC].partition_broadcast(64), ("z", c)))

            def cont():
                pb = k.next_s()
                k.mm(pb[0:64, :], ones32[64:65, 0:64], r[64:65, :])
                k.act(bc1[i], pb[0:64, :], AF.Copy)
                y = yo[i]
                if gate_row is not None:
                    k.tt(t1[i % 2][0:64, :], accb[0:64, :], bc1[i], ALU.mult)
                    k.tt(y[0:64, :], t1[i % 2][0:64, :], bc2[i], ALU.mult)
                else:
                    k.tt(y[0:64, :], accb[0:64, :], bc1[i], ALU.mult)
                k.dma("pool", DR(dst[base:base + 64, c * TC:(c + 1) * TC], dkey), y[0:64, :])
            if defer:
                return cont
            cont()
            return None

        accn = [0]

        def gqa(qblk0, kblk, vi, strips, kts_fn, diag_fn, dst, dkeyf, sinks=False, gate_j=None, bias0=0, extra_fn=None):
            k.dma("sp", kTb, DR(zf[kblk], *zall))
            load_v(vi)
            ncol = strips.shape[-1]
            for b in range(4):
                q_ = qT[b % 2]
                k.dma("sp", q_, DR(zf[qblk0 + b], *zall))
                for half in range(2):
                    hh = b + 4 * half
                    base = 64 * half
                    st_ = stp[hh % 2]
                    k.dma("sp", st_[:, 0:ncol], DR(strips[hh], "tab"))
                    steps = []
                    for c in range(NCH):
                        acc = k.banks[4 + accn[0] % 4]
                        accn[0] += 1
                        kts = kts_fn(c)
                        for kt in kts:
                            qk = [(kTb[base:base + 64, kt * 128:(kt + 1) * 128], q_[base:base + 64, c * TC:(c + 1) * TC])]
                            if extra_fn is not None:
                                qk.append(extra_fn(kt, c, half))
                            bias = None
                            if diag_fn(kt, c):
                                off = (512 * c - 128 * kt) + 384
                                qk.append((ident16, st_[:, off:off + 512]))
                            else:
                                qk.append((ident16, st_[:, 768:1280]))
                            first, last = (kt == kts[0]), (kt == kts[-1])
                            vo_ = (kt * 2 + half) * 65
                            stp_ = dict(qk=qk, nk=128, bias=bias, pv=[(acc, Va[:, vo_:vo_ + 128], first, last)])
                            if last:
                                def after(acc=acc, base=base, hh=hh, c=c, b=b):
                                    return finalize(acc, base, hh, c, dst[b], dkeyf(c),
                                                    gate_row=(hh * 3 + gate_j) if gate_j is not None else None,
                                                    sink=(esink[64:65, l * 8 + hh: l * 8 + hh + 1] if sinks else None), defer=True)
                                stp_["after"] = after
                            steps.append(stp_)
                    run_attn(steps, Pb)

        if "C" in P2:
            gqa(17, 21, 2, b_stc, lambda c: [kt for kt in range(4 * c - 1, 4 * c + 4) if kt >= 0], lambda kt, c: True,
                ycT, lambda c: ("yc", c), sinks=True)

        if "W" in P2:
            gqa(8, 15, 1, b_stbw, lambda c: [kt for kt in range(4 * c - 4, 4 * c + 4) if kt >= 0], lambda kt, c: True,
                ybT[2], lambda c: ("yb", 2, c), gate_j=2)

        wk = ("w", l)
        w1sb = k.alloc(32 * 256, BF16)
        w1v = T(w1sb.ap.rearrange("p (a b) -> p a b", a=32), w1sb.res)
        w2p = k.alloc(2 * 2 * 128, BF16)
        gel = [[k.alloc(256, BF16) for _ in range(2)] for _ in range(2)]
        xg = k.alloc(256, F32)
        ug = k.alloc(256, F32)
        cvec = k.alloc(8, F32)
        kcmpT = k.alloc(256, BF16)
        vcmp = k.alloc(2 * 2 * 65, BF16)
        ctab = [k.alloc(NCH * 512, BF16) for _ in range(2)]
        impsum = [k.alloc(32 * 64, F32) for _ in range(2)]
        keepA = k.alloc(32 * 64, F32)
        addA = k.alloc(32 * 64, F32)
        selT = [k.alloc(S, BF16) for _ in range(2)]
        expand16 = k.alloc(32 * 128, BF16)
        k.memset(expand16, 0.0)
        k.memset(selT[0], 0.0)
        k.memset(selT[1], 0.0)
        vsel = [k.alloc(64, F32) for _ in range(2)]
        v2sel = k.alloc(64, F32)
        m8a = k.alloc(8, F32)
        m8b = k.alloc(8, F32)
        mneg = [k.alloc(64, BF16) for _ in range(2)]
        recq = k.alloc(8, F32)
        k.dma("pool", expand16[0:64, :], DR(i_expand.rearrange("p a b -> p (a b)"), "in"))
        k.dma("sp", keepA, DR(i_keep.rearrange("p a b -> p (a b)"), "in"))
        k.dma("sp", addA, DR(i_add.rearrange("p a b -> p (a b)"), "in"))
        GC = 2.0 * math.sqrt(2.0 / math.pi)
        k.memset(vcmp, 1.0)
        for kv in (range(2) if "M" in P2 else []):
            for half in range(2):
                for a8 in range(8):
                    k.dma("sp", w1v[half * 64:(half + 1) * 64, a8 * 4:(a8 + 1) * 4, :],
                          DR(b_w1[l, kv, a8 * 256:(a8 + 1) * 256, :].rearrange("(a d) j -> d a j", d=64), wk))
            k.memset(w2p, 0.0)
            for g in range(2):
                for jc in range(2):
                    o_ = (jc * 2 + g) * 128 + g * 64
                    k.dma("sp", w2p[:, o_:o_ + 64], DR(b_w2[l, kv, jc * 128:(jc + 1) * 128, :], wk))
            k.dma("sp", kTb, DR(zf[12 + kv], *zall))
            for jc in range(2):
                ps = k.next_s()
                for a in range(32):
                    pc = (l * 2 + kv) * 32 + a
                    k.mm(ps[:, 0:1], w1sb[0:64, a * 256 + jc * 128: a * 256 + jc * 128 + 128], pos16[0:64, pc:pc + 1],
                         start=(a == 0), stop=(a == 31))
                k.copy(cvec[:, kv * 2 + jc: kv * 2 + jc + 1], ps[:, 0:1])
            for g in range(2):
                for jc in range(2):
                    ps = k.next_s()
                    for a in range(32):
                        k.mm(ps[:, 0:255], w1sb[g * 64:(g + 1) * 64, a * 256 + jc * 128: a * 256 + jc * 128 + 128],
                             kTb[g * 64:(g + 1) * 64, a: a + 16 * 254 + 1: 16], start=(a == 0), stop=(a == 31))
                    k.ts(xg[:, 0:255], ps[:, 0:255], cvec[:, kv * 2 + jc: kv * 2 + jc + 1], None, ALU.add)
                    k.tt(ug[:, 0:255], xg[:, 0:255], xg[:, 0:255], ALU.mult)
                    k.ts(ug[:, 0:255], ug[:, 0:255], 0.044715, 1.0, ALU.mult, ALU.add)
                    k.tt(ug[:, 0:255], ug[:, 0:255], xg[:, 0:255], ALU.mult)
                    k.act(ug[:, 0:255], ug[:, 0:255], AF.Sigmoid, scale=GC)
                    k.tt(gel[g][jc][:, 0:255], ug[:, 0:255], xg[:, 0:255], ALU.mult)
            if kv == 0:
                ps = k.next_s()
                n_ = 0
                for g in range(2):
                    for jc in range(2):
                        o_ = (jc * 2 + g) * 128
                        k.mm(ps[:, 0:255], w2p[:, o_:o_ + 128], gel[g][jc][:, 0:255], start=(n_ == 0), stop=(n_ == 3))
                        n_ += 1
                k.copy(kcmpT[:, 0:255], ps[:, 0:255])
            else:
                for nt in range(2):
                    nk = 128 if nt == 0 else 127
                    for g in range(2):
                        ps = k.next_s()
                        for jc in range(2):
                            o_ = (jc * 2 + 0) * 128
                            k.mm(ps[0:nk, 0:64], gel[g][jc][:, nt * 128: nt * 128 + nk], w2p[:, o_:o_ + 64], start=(jc == 0), stop=(jc == 1))
                        o2 = (nt * 2 + g) * 65
                        k.copy(vcmp[0:nk, o2:o2 + 64], ps[0:nk, 0:64])

        jobn = [0]
        for b in (range(4) if "M" in P2 else []):
            q_ = qT[b % 2]
            k.dma("sp", q_, DR(zf[8 + b], *zall))
            for half in range(2):
                hh = b + 4 * half
                g = half
                base = 64 * half
                for nt in range(2):
                    k.dma("sp", T(ctab[nt].ap.rearrange("p (c q) -> p c q", c=NCH), ctab[nt].res),
                          DR(b_cmpt[hh, nt].rearrange("c p q -> p c q"), "tab"))
                steps = []
                for c in range(NCH):
                    acc = k.banks[4 + (jobn[0] % 2) * 2]
                    impb = k.banks[5 + (jobn[0] % 2) * 2]
                    jobn[0] += 1
                    nts = [0] if c < 4 else [0, 1]
                    for nt in nts:
                        nk = 128 if nt == 0 else 127
                        first, last = (nt == nts[0]), (nt == nts[-1])
                        qk = [(kcmpT[base:base + 64, nt * 128: nt * 128 + nk], q_[base:base + 64, c * TC:(c + 1) * TC]),
                              (ident16[:, 0:nk], ctab[nt][:, c * 512:(c + 1) * 512])]
                        o2 = (nt * 2 + g) * 65

                        def pv2(P, Pprev, nk=nk, nt=nt, impb=impb, first=first, last=last):
                            if not last:
                                return
                            for qt in range(4):
                                reg = impb[:, qt * 65:(qt + 1) * 65]
                                if first:
                                    k.mm(reg, P[0:nk, qt * 128:(qt + 1) * 128], ov16[0:nk, nt * 65:(nt + 1) * 65], start=True, stop=True)
                                else:
                                    k.mm(reg, Pprev[0:128, qt * 128:(qt + 1) * 128], ov16[0:128, 0:65], start=True, stop=False)
                                    k.mm(reg, P[0:nk, qt * 128:(qt + 1) * 128], ov16[0:nk, nt * 65:(nt + 1) * 65], start=False, stop=True)
                        stp_ = dict(qk=qk, nk=nk, bias=None, pv=[(acc[0:65, :], vcmp[0:nk, o2:o2 + 65], first, last)], pv2=pv2)
                        if last:
                            def after(acc=acc, impb=impb, base=base, hh=hh, c=c, b=b, g=g):
                                finalize(acc, base, hh, c, ybT[0][b], ("yb", 0, c), gate_row=hh * 3 + 0)
                                for qt in range(4):
                                    k.ts(recq[:, qt:qt + 1], impb[:, qt * 65 + 64: qt * 65 + 65], 1e-30, None, ALU.max)
                                k.recip(recq[:, 0:4], recq[:, 0:4])
                                for qt in range(4):
                                    d_ = impsum[g][:, (c * 4 + qt) * 64:(c * 4 + qt + 1) * 64]
                                    if b == 0:
                                        k.ts(d_, impb[:, qt * 65: qt * 65 + 64], recq[:, qt:qt + 1], None, ALU.mult)
                                    else:
                                        k.stt(d_, impb[:, qt * 65: qt * 65 + 64], recq[:, qt:qt + 1], d_, ALU.mult, ALU.add)
                            stp_["after"] = after
                        steps.append(stp_)
                run_attn(steps, Pb)

        for g in (range(2) if "M" in P2 else []):
            for qt in range(32):
                v = vsel[qt % 2]
                k.tt(v, impsum[g][:, qt * 64:(qt + 1) * 64], keepA[:, qt * 64:(qt + 1) * 64], ALU.mult)
                k.tt(v, v, addA[:, qt * 64:(qt + 1) * 64], ALU.add)
                k.max8(m8a, v)
                k.match_replace(v2sel, m8a, v, -3e9)
                k.max8(m8b, v2sel)
                k.ts(v2sel, v, m8b[:, 7:8], None, ALU.is_ge)
                mn = mneg[qt % 2]
                k.ts(mn, v2sel, -1.0, -NEG, ALU.add, ALU.mult)
                ps = k.next_s()
                k.mm(ps[0:64, 0:128], mn, ident16)
                k.copy(selT[g][0:64, qt * 128:(qt + 1) * 128], ps[0:64, 0:128])

        if debug and "M" in P2:
            for g in range(2):
                k.dma("pool", DR(dbg_sel[g], "dbg"), selT[g][0:64, :])
                k.dma("pool", DR(dbg_imp[g], "dbg"), impsum[g])
        if "S" in P2:
          gqa(8, 14, 0, b_stbs, lambda c: list(range(0, 4 * c + 4)), lambda kt, c: kt >= 4 * c - 1,
            ybT[1], lambda c: ("yb", 1, c), gate_j=1, bias0=4,
            extra_fn=lambda kt, c, half: (expand16[:, kt * 128:(kt + 1) * 128], selT[half][:, c * TC:(c + 1) * TC]))
        k.release(m0)


    phase2_impl = phase2

    def phase3(l, s, last):
        m0 = k.mark()
        tl = {}
        tl["h8"] = [[k.alloc(512, F32) for _ in range(8)] for _ in range(2)]
        tl["sq"] = [k.alloc(512, BF16) for _ in range(8)]
        tl["nT"] = [[k.alloc(512, BF16) for _ in range(8)] for _ in range(2)]
        tl["rstd"] = [k.alloc(512, F32) for _ in range(2)]
        tl["sg"] = [k.alloc(512, F32) for _ in range(2)]
        m1 = k.mark()
        tl["act"] = [k.alloc(512, BF16) for _ in range(22)]
        tl["wi"] = [k.alloc(8 * 256, BF16) for _ in range(4)]
        tl["wo"] = [k.alloc(22 * 256, BF16) for _ in range(2)]
        m2 = k.mark()
        k.release(m1)
        ytile = [k.alloc(512, BF16) for _ in range(20)]
        gt = [k.alloc(512, BF16) for _ in range(6)]
        wbt = [k.alloc(4 * 128, BF16) for _ in range(6)]
        mg = [k.alloc(512, BF16) for _ in range(8)]
        mtmp = [k.alloc(512, F32) for _ in range(2)]
        wot = [k.alloc(8 * 128, BF16) for _ in range(3)]
        pt = [k.alloc(2 * 512, BF16) for _ in range(1)]
        gsig = [k.alloc(512, F32) for _ in range(2)]
        k.release(max(m2, k.mark()))
        wk = ("w", l)
        wbr = b_wbr[l]
        wout = b_wout[l].rearrange("(kc p) c -> p kc c", p=128)
        wpg = b_wpg[l].rearrange("(kc p) c -> p kc c", p=128)
        wple = b_wple[l].rearrange("(kc p) c -> p kc c", p=128)
        for c in range(NCH):
            h8 = tl["h8"][c % 2]
            tok = slice(c * TC, (c + 1) * TC)
            hkey = ("h", s, c)
            for kc in range(8):
                k.dma("sp", h8[kc], DR(hT[s, kc * 128:(kc + 1) * 128, tok], hkey))
            for i in range(4):
                k.dma("sp", ytile[i], DR(yaT[i, :, tok], ("ya", c)))
                for j in range(3):
                    k.dma("sp", ytile[4 + 4 * j + i], DR(ybT[j, i, :, tok], ("yb", j, c)))
                k.dma("sp", ytile[16 + i], DR(ycT[i, :, tok], ("yc", c)))
            kch = []
            for i in range(4):
                kch.append((i, 0, [(128 * i, 128)]))
            for j in range(3):
                for i in range(4):
                    kch.append((4 + 4 * j + i, 1, [(64 * i, 64), (64 * (i + 4), 64)]))
            for i in range(4):
                kch.append((16 + i, 2, [(64 * i, 64), (64 * (i + 4), 64)]))
            for ob in range(8):
                for m in range(3):
                    k.dma("sp", gt[(ob % 2) * 3 + m], DR(zf[22 + 8 * m + ob, :, tok], ("z", c)))
                wb = [wbt[(ob % 2) * 3 + m] for m in range(3)]
                w3_ = T(wb[0].ap.rearrange("p (a b) -> p a b", a=4), wb[0].res)
                k.dma("sp", w3_, DR(wbr[0, :, ob * 128:(ob + 1) * 128].rearrange("(i p) c -> p i c", p=128), wk))
                for m in (1, 2):
                    w3_ = T(wb[m].ap.rearrange("p (a b) -> p a b", a=4), wb[m].res)
                    for hf in range(2):
                        k.dma("sp", w3_[hf * 64:(hf + 1) * 64, :, :],
                              DR(wbr[m, hf * 256:(hf + 1) * 256, ob * 128:(ob + 1) * 128].rearrange("(i p) c -> p i c", p=64), wk))
                for m in range(3):
                    ps = k.next_s()
                    mine = [x for x in kch if x[1] == m]
                    for xi, (yi, _, rows) in enumerate(mine):
                        i4 = xi % 4
                        k.mm(ps, wb[m][:, i4 * 128:(i4 + 1) * 128], ytile[yi], start=(xi == 0), stop=(xi == len(mine) - 1))
                    g = gt[(ob % 2) * 3 + m]
                    if m == 0:
                        k.tt(mtmp[ob % 2], ps, g, ALU.mult)
                    elif m == 1:
                        k.tt(gsig[ob % 2], ps, g, ALU.mult)
                        k.tt(mtmp[ob % 2], mtmp[ob % 2], gsig[ob % 2], ALU.add)
                    else:
                        k.tt(gsig[ob % 2], ps, g, ALU.mult)
                        k.tt(mg[ob], mtmp[ob % 2], gsig[ob % 2], ALU.add)
            for ob in range(8):
                w = wot[ob % 3]
                k.dma("sp", T(w.ap.rearrange("p (a b) -> p a b", a=8), w.res), DR(wout[:, :, ob * 128:(ob + 1) * 128], wk))
                ps = k.next_s()
                for kc in range(8):
                    k.mm(ps, w[:, kc * 128:(kc + 1) * 128], mg[kc], start=(kc == 0), stop=(kc == 7))
                k.tt(h8[ob], h8[ob], ps, ALU.add)
            if debug and l == 0 and s == 0:
                for kc in range(8):
                    k.dma("pool", DR(dbg_h[1, kc * 128:(kc + 1) * 128, tok], "dbg"), h8[kc], is_out=True)
            nT = tl["nT"][c % 2]
            rstd = tl["rstd"][c % 2]
            rmsnorm(h8, lambda kc: ng_sb[:, (l * 4 + 2) * 8 + kc:(l * 4 + 2) * 8 + kc + 1], nT, tl["sq"], rstd)
            ffn(l, 1, h8, nT, tl)
            if debug and l == 0 and s == 0:
                for kc in range(8):
                    k.dma("pool", DR(dbg_h[2, kc * 128:(kc + 1) * 128, tok], "dbg"), h8[kc], is_out=True)
            nT2 = tl["nT"][(c + 1) % 2]
            rmsnorm(h8, lambda kc: ng_sb[:, (l * 4 + 3) * 8 + kc:(l * 4 + 3) * 8 + kc + 1], nT2, tl["sq"], rstd)
            p_ = pt[0]
            k.dma("pool", T(p_.ap.rearrange("p (a b) -> p a b", a=2), p_.res),
                  DR(pT[l, s, :, tok].rearrange("(a p) t -> p a t", p=128), "in"))
            for ob in range(8):
                w = wot[ob % 3]
                k.dma("sp", T(w.ap.rearrange("p (a b) -> p a b", a=8), w.res), DR(wpg[:, :, ob * 128:(ob + 1) * 128], wk))
                ps = k.next_s()
                for kc in range(8):
                    k.mm(ps, w[:, kc * 128:(kc + 1) * 128], nT2[kc], start=(kc == 0), stop=(kc == 7))
                gs = gsig[ob % 2]
                k.act(gs, ps, AF.Sigmoid)
                w2 = wbt[ob % 6]
                k.dma("sp", T(w2.ap[:, 0:256].rearrange("p (a b) -> p a b", a=2), w2.res), DR(wple[:, :, ob * 128:(ob + 1) * 128], wk))
                ps2 = k.next_s()
                for kc in range(2):
                    k.mm(ps2, w2[:, kc * 128:(kc + 1) * 128], p_[:, kc * 512:(kc + 1) * 512], start=(kc == 0), stop=(kc == 1))
                k.tt(gs, gs, ps2, ALU.mult)
                k.tt(h8[ob], h8[ob], gs, ALU.add)
            if last:
                rmsnorm(h8, lambda kc: fn_sb[:, kc:kc + 1], tl["h8"][(c + 1) % 2], tl["sq"], rstd)
                for kc in range(8):
                    k.dma("pool", DR(yT[s, kc * 128:(kc + 1) * 128, tok], "out"), tl["h8"][(c + 1) % 2][kc], is_out=True)
            else:
                for kc in range(8):
                    k.dma("pool", DR(hT[s, kc * 128:(kc + 1) * 128, tok], hkey), h8[kc])
        k.release(m0)

    cast_weights(0)
    for l in range(L):
        for s in range(NS):
            if 1 in phases:
                phase1(l, s)
            if s == 0 and l + 1 < L:
                cast_weights(l + 1)
            if 2 in phases:
                phase2_impl(l, s)
            if 3 in phases:
                phase3(l, s, last=(l == L - 1))
    print("ops per engine:", {e: len(q) for e, q in k.s.q.items()}, flush=True)
    k.s.emit(nc, es)


def prep_common(inp, L):
    f = lambda a: np.ascontiguousarray(np.asarray(a, np.float32))
    ng = f(inp["norm_g"])[:L].reshape(L, 4, 8, 128).transpose(3, 0, 1, 2).reshape(128, L * 4 * 8)
    fng = f(inp["final_norm"]).reshape(8, 128).T
    dlam = np.broadcast_to(f(inp["diff_lambda"])[:L].reshape(1, L * 4 * 64), (128, L * 4 * 64))
    subln = f(inp["diff_subln"])[:L].T
    pos = f(inp["nsa_cmp_pos"])[:L]
    posT = pos.transpose(3, 0, 1, 2).reshape(64, L * 2 * 32)
    posT = np.concatenate([posT, posT], axis=0)
    sinks = np.broadcast_to(f(inp["swa_sinks"])[:L].reshape(1, L * 8), (128, L * 8))
    rb = f(inp["rel_bias"])
    rb31 = np.broadcast_to(rb[31:32, :], (128, 20))
    st_a, st_bs, st_bw, st_c, cmp_t = host_tables(rb)
    ov, expand, keep, add, ident = host_consts()
    c = {
        "ng": ng, "fng": fng, "dlam": dlam, "subln": subln, "posT": posT, "sinks": sinks, "rb31": rb31,
        "st_a": st_a, "st_bs": st_bs, "st_bw": st_bw, "st_c": st_c, "cmp_t": cmp_t,
        "ov": ov, "expand": expand, "keep": keep, "addc": add, "ident": ident,
        "ffn1_wi": f(inp["ffn1_wi"])[:L], "ffn2_wi": f(inp["ffn2_wi"])[:L],
        "ffn1_wo": f(inp["ffn1_wo"])[:L], "ffn2_wo": f(inp["ffn2_wo"])[:L],
        "w_in": f(inp["w_in"])[:L], "cmp_w1": f(inp["nsa_cmp_w1"])[:L], "cmp_w2": f(inp["nsa_cmp_w2"])[:L],
        "w_branch": f(inp["w_branch"])[:L], "w_out": f(inp["w_out"])[:L], "w_ple": f(inp["w_ple"])[:L],
        "w_ple_gate": f(inp["w_ple_gate"])[:L],
    }
    return {k_: np.ascontiguousarray(v) for k_, v in c.items()}


def prep_core(inp, seqs, L):
    x = np.asarray(inp["x"], np.float32)
    p = np.asarray(inp["p"], np.float32)
    xT = np.ascontiguousarray(x[seqs].transpose(0, 2, 1))
    pT = np.ascontiguousarray(p[:L][:, seqs].transpose(0, 1, 3, 2))
    return {"xT": xT, "pT": pT}


_PROG = {}


def kernel(**inputs):
    NCORES, NS, L = 8, 2, 4
    key = (NS, L)
    if key not in _PROG:
        _PROG[key] = build_program(NS, L)
    nc = _PROG[key]
    common = prep_common(inputs, L)
    in_maps = []
    for ci in range(NCORES):
        m = dict(common)
        m.update(prep_core(inputs, list(range(ci * NS, (ci + 1) * NS)), L))
        in_maps.append(m)
    res = run_bass_kernel_spmd(nc, in_maps, core_ids=list(range(NCORES)))
    out = np.empty((NCORES * NS, S, D), np.float32)
    for ci in range(NCORES):
        yT = res.results[ci]["yT"]
        for j in range(NS):
            out[ci * NS + j] = yT[j].T
    return out
```

```python
import contextlib
import math
import os
import numpy as np
import concourse.bass as bass
import concourse.mybir as mybir
from concourse.bass_utils import run_bass_kernel_spmd

F32 = mybir.dt.float32
BF16 = mybir.dt.bfloat16
AF = mybir.ActivationFunctionType
ALU = mybir.AluOpType
AX = mybir.AxisListType

D = 1024
S = 4096
FF = 2816
PLE = 256
DIN = 6680
TC = 512
NCH = S // TC
NEG = -30000.0
EPS = 1e-6
NDMA = 8
CLEAR_ENG = os.environ.get('CLEAR_ENG', 'pool')
EPOCH = int(os.environ.get('EPOCH', 7000))
DEP = int(os.environ.get('DEP', 400))
SLOT = 256

C_AQ, C_AK, C_AV, C_BQ, C_BKC, C_BVC, C_BKS, C_BVS, C_BKW, C_BVW, C_BG = 0, 512, 1024, 1536, 2048, 2176, 2304, 2432, 2560, 2688, 2816
C_CQ, C_CK, C_CV, C_GA, C_GB, C_GC = 2840, 3352, 3480, 3608, 4632, 5656


class Res:
    __slots__ = ("w", "rs", "multi")

    def __init__(self, multi=False):
        self.w = {}
        self.rs = {}
        self.multi = multi


class T:
    __slots__ = ("ap", "res")

    def __init__(self, ap, res):
        self.ap = ap
        self.res = res

    def __getitem__(self, key):
        return T(self.ap[key], self.res)


class Sched:
    ENGS = ("pe", "act", "dve", "pool", "sp")

    def __init__(self):
        self.q = {e: [] for e in self.ENGS}
        self.cnt = {}
        self.bdone = {}
        self.known = {e: {} for e in self.ENGS}
        self.pending = {e: {} for e in self.ENGS}
        self.awaiting = {e: [] for e in self.ENGS}
        self.first_after = {}
        self.last_tok = {e: None for e in self.ENGS}
        self.dnext = {e: 0 for e in self.ENGS}
        self.nwaits = 0
        self.clear_tok = {}

    @staticmethod
    def _ep(stream):
        return EPOCH if len(stream) == 1 else DEP

    def _boundary(self, stream):
        n = self.cnt.get(stream, 0)
        EP = self._ep(stream)
        if n == 0 or n % EP != 0 or self.bdone.get(stream, 0) >= n // EP:
            return
        k = n // EP
        self.bdone[stream] = k
        for X in self.ENGS:
            self.pending[X][stream] = n
        if k >= 2:
            proofs = []
            fa = self.first_after.pop((stream, k - 1))
            for X in self.ENGS:
                p = fa.get(X) or self.last_tok[X]
                if p is not None:
                    proofs.append(p)
            self.clear_tok[(stream, k)] = self._append(CLEAR_ENG, ("clear", stream, (k - 2) % 3), proofs)
        self.first_after[(stream, k)] = {}
        ct = self.clear_tok.pop((stream, k - 1), None)
        for X in self.ENGS:
            if ct is not None and self.pending[X].get(ct[0], 0) < ct[1]:
                self.pending[X][ct[0]] = ct[1]
            self.awaiting[X].append((stream, k))

    def _append(self, eng, fn, needs, dma=False, notoken=False):
        if dma:
            slot = self.dnext[eng] % NDMA
            self.dnext[eng] += 1
            stream = (eng, slot)
        else:
            stream = (eng,)
        if not notoken:
            self._boundary(stream)
        waits = {}
        for tok in needs:
            if tok is None:
                continue
            st, c = tok
            if eng == "pe" and st == ("pe",):
                continue
            key = (st, (c - 1) // self._ep(st))
            if waits.get(key, 0) < c:
                waits[key] = c
        for st, c in self.pending[eng].items():
            key = (st, (c - 1) // self._ep(st))
            if waits.get(key, 0) < c:
                waits[key] = c
        self.pending[eng] = {}
        if dma:
            prev = self.cnt.get(stream, 0)
            if prev > 0:
                key = (stream, (prev - 1) // DEP)
                if waits.get(key, 0) < prev:
                    waits[key] = prev
        kn = self.known[eng]
        wl = []
        for key in sorted(waits.keys()):
            st = key[0]
            c = waits[key]
            if kn.get(st, 0) >= c:
                continue
            EP = self._ep(st)
            assert (c - 1) // EP >= self.bdone.get(st, 0) - 1, ("stale token", eng, st, c, self.cnt.get(st, 0))
            kn[st] = c
            wl.append((st, c))
        self.nwaits += len(wl)
        if notoken:
            self.q[eng].append((wl, fn, None))
            return None
        n = self.cnt.get(stream, 0)
        self.cnt[stream] = n + 1
        tok = (stream, n + 1)
        for key in self.awaiting[eng]:
            fa = self.first_after.get(key)
            if fa is not None:
                fa[eng] = tok
        self.awaiting[eng] = []
        self.last_tok[eng] = tok
        self.q[eng].append((wl, fn, tok))
        return tok

    def op(self, eng, fn, reads=(), writes=(), dma=False, is_out=False):
        needs = []
        for r in reads:
            needs.extend(r.w.items())
        for r in writes:
            if not (r.multi and dma):
                needs.extend(r.w.items())
            needs.extend(r.rs.items())
        tok = self._append(eng, fn, needs, dma=dma)
        st, c = tok
        for r in reads:
            if r.rs.get(st, 0) < c:
                r.rs[st] = c
        for r in writes:
            if r.multi and dma and not r.rs:
                if r.w.get(st, 0) < c:
                    r.w[st] = c
            else:
                r.w = {st: c}
                r.rs = {}
        return tok

    def emit(self, nc, es):
        last = [(st, c) for st, c in self.cnt.items() if len(st) == 2]
        self._append("sp", ("final",), last, notoken=True)
        sems = {}
        for st in sorted(self.cnt.keys()):
            for i in range(min(3, (self.cnt[st] - 1) // self._ep(st) + 1)):
                sems[(st, i)] = es.enter_context(nc.semaphore("s_" + "_".join(str(x) for x in st) + "_%d" % i))
        print("semaphores:", len(sems), "waits:", self.nwaits, flush=True)

        def semval(tok):
            st, c = tok
            EP = self._ep(st)
            ep = (c - 1) // EP
            return sems[(st, ep % 3)], (c - ep * EP) * (16 if len(st) == 2 else 1)

        block = es.enter_context(nc.Block())

        def run(e, eng):
            for wl, fn, tok in self.q[e]:
                if isinstance(fn, tuple):
                    for t in wl:
                        sm, v = semval(t)
                        eng.wait_ge(sm, v)
                    if fn[0] == "clear":
                        eng.sem_clear(sems[(fn[1], fn[2])])
                        own, _ = semval(tok)
                        eng.nop(nofuse=True).then_inc(own, 1)
                    continue
                for t in wl[:-1]:
                    sm, v = semval(t)
                    eng.wait_ge(sm, v)
                ins = fn(eng)
                if wl:
                    sm, v = semval(wl[-1])
                    ins._wait_ge(sm, v)
                own, _ = semval(tok)
                ins.then_inc(own, 16 if len(tok[0]) == 2 else 1)

        @block.tensor
        def _(eng):
            run("pe", eng)

        @block.scalar
        def _(eng):
            run("act", eng)

        @block.vector
        def _(eng):
            run("dve", eng)

        @block.gpsimd
        def _(eng):
            run("pool", eng)

        @block.sync
        def _(eng):
            run("sp", eng)


def _res_of(*ts):
    out = []
    for t in ts:
        if t is not None and not isinstance(t, (int, float)):
            out.extend(t.res)
    return out


def _ap(t):
    return t.ap if isinstance(t, T) else t


class K:
    def __init__(self, nc, es, arena_slots, n_static_f32):
        self.nc = nc
        self.es = es
        self.s = Sched()
        self.arena = es.enter_context(nc.sbuf_tensor("arena", [128, arena_slots * SLOT], F32))
        self.arena16 = self.arena[:, :].bitcast(BF16)
        self.ares = [Res() for _ in range(arena_slots)]
        self.nslots = arena_slots
        self.top = 0
        self.static = es.enter_context(nc.sbuf_tensor("static", [128, n_static_f32], F32))
        self.static16 = self.static[:, :].bitcast(BF16)
        self.stop_ = 0
        self.nstatic = n_static_f32
        self.banks = [T(es.enter_context(nc.psum_tensor("ps%d" % i, [128, 512], F32))[:, :], [Res()]) for i in range(8)]
        self.sring = 0
        self.dram_res = {}

    def mark(self):
        return self.top

    def release(self, m):
        self.top = m

    def alloc(self, n, dt=BF16, parts=128):
        nf = (n + 1) // 2 if dt == BF16 else n
        ns = (nf + SLOT - 1) // SLOT
        assert self.top + ns <= self.nslots, "arena overflow %d" % (self.top + ns)
        s0 = self.top
        self.top += ns
        res = self.ares[s0:s0 + ns]
        if dt == BF16:
            ap = self.arena16[0:parts, s0 * SLOT * 2: s0 * SLOT * 2 + n]
        else:
            ap = self.arena[0:parts, s0 * SLOT: s0 * SLOT + n]
        return T(ap, res)

    def salloc(self, n, dt=F32, parts=128):
        nf = (n + 1) // 2 if dt == BF16 else n
        nf = (nf + 7) // 8 * 8
        assert self.stop_ + nf <= self.nstatic, "static overflow"
        o = self.stop_
        self.stop_ += nf
        if dt == BF16:
            ap = self.static16[0:parts, o * 2: o * 2 + n]
        else:
            ap = self.static[0:parts, o: o + n]
        return T(ap, [Res()])

    def dres(self, key):
        r = self.dram_res.get(key)
        if r is None:
            r = self.dram_res[key] = Res(multi=True)
        return r

    def next_s(self):
        b = self.banks[self.sring % 4]
        self.sring += 1
        return b

    def dma(self, q, out, in_, is_out=False):
        o, i = _ap(out), _ap(in_)
        self.s.op(q, lambda e: e.dma_start(out=o, in_=i), reads=_res_of(in_), writes=_res_of(out), dma=True, is_out=is_out)

    def mm(self, out, lhsT, rhs, start=True, stop=True):
        o, l, r = _ap(out), _ap(lhsT), _ap(rhs)
        self.s.op("pe", lambda e: e.matmul(o, lhsT=l, rhs=r, start=start, stop=stop), reads=_res_of(lhsT, rhs), writes=_res_of(out))

    def act(self, out, in_, func, bias=None, scale=1.0, eng="act"):
        o, i = _ap(out), _ap(in_)
        kw = {}
        if bias is not None:
            kw["bias"] = _ap(bias)
        sc = _ap(scale)
        self.s.op(eng, lambda e: e.activation(out=o, in_=i, func=func, scale=sc, **kw),
                  reads=_res_of(in_, bias, scale), writes=_res_of(out))

    def ts(self, out, in0, s1, s2, op0, op1=None, eng="dve"):
        o, i = _ap(out), _ap(in0)
        a1, a2 = _ap(s1), _ap(s2)
        if op1 is None:
            f = lambda e: e.tensor_scalar(out=o, in0=i, scalar1=a1, scalar2=None, op0=op0)
        else:
            f = lambda e: e.tensor_scalar(out=o, in0=i, scalar1=a1, scalar2=a2, op0=op0, op1=op1)
        self.s.op(eng, f, reads=_res_of(in0, s1, s2), writes=_res_of(out))

    def tt(self, out, in0, in1, op, eng="dve"):
        o, a, b = _ap(out), _ap(in0), _ap(in1)
        self.s.op(eng, lambda e: e.tensor_tensor(out=o, in0=a, in1=b, op=op), reads=_res_of(in0, in1), writes=_res_of(out))

    def stt(self, out, in0, scalar, in1, op0, op1, eng="dve"):
        o, a, b, sc = _ap(out), _ap(in0), _ap(in1), _ap(scalar)
        self.s.op(eng, lambda e: e.scalar_tensor_tensor(out=o, in0=a, scalar=sc, in1=b, op0=op0, op1=op1),
                  reads=_res_of(in0, in1, scalar), writes=_res_of(out))

    def copy(self, out, in_, eng="dve"):
        o, i = _ap(out), _ap(in_)
        self.s.op(eng, lambda e: e.tensor_copy(out=o, in_=i), reads=_res_of(in_), writes=_res_of(out))

    def recip(self, out, in_):
        o, i = _ap(out), _ap(in_)
        self.s.op("dve", lambda e: e.reciprocal(out=o, in_=i), reads=_res_of(in_), writes=_res_of(out))

    def memset(self, out, val, eng="dve"):
        o = _ap(out)
        self.s.op(eng, lambda e: e.memset(o, val), writes=_res_of(out))

    def reduce_sum(self, out, in_):
        o, i = _ap(out), _ap(in_)
        self.s.op("dve", lambda e: e.reduce_sum(out=o, in_=i, axis=AX.X), reads=_res_of(in_), writes=_res_of(out))

    def max8(self, out, in_):
        o, i = _ap(out), _ap(in_)
        self.s.op("dve", lambda e: e.max(out=o, in_=i), reads=_res_of(in_), writes=_res_of(out))

    def match_replace(self, out, m8, vals, imm):
        o, m, v = _ap(out), _ap(m8), _ap(vals)
        self.s.op("dve", lambda e: e.match_replace(out=o, in_to_replace=m, in_values=v, imm_value=imm),
                  reads=_res_of(m8, vals), writes=_res_of(out))


def rel_bucket_np(dist):
    n = np.maximum(dist, 0)
    nf = np.maximum(n, 1).astype(np.float32)
    large = 16 + (np.log(nf / np.float32(16)) / np.float32(math.log(8.0)) * np.float32(16)).astype(np.int32)
    large = np.minimum(large, 31)
    return np.where(n < 16, n, large)


def host_tables(rel_bias):
    rb = np.asarray(rel_bias, np.float32)
    kk = np.arange(128)[:, None]
    def strip(ncols, heads, wmax):
        j = np.arange(ncols)[None, :]
        dist = j - kk - 384
        ok = (dist >= 0) if wmax is None else ((dist >= 0) & (dist < wmax))
        b = rel_bucket_np(dist)
        out = np.empty((len(heads), 128, ncols), np.float32)
        for i, h in enumerate(heads):
            out[i] = np.where(ok, rb[b, h], np.float32(NEG))
        return out
    st_a = strip(1280, range(0, 4), None)
    st_bs = strip(1280, range(4, 12), None)
    st_bw = strip(1408, range(4, 12), 512)
    st_c = strip(1024, range(12, 20), 128)
    cmp_t = np.full((8, 2, NCH, 128, 512), np.float32(NEG), np.float32)
    for nt in range(2):
        n = nt * 128 + np.arange(128)[:, None]
        for c in range(NCH):
            q = c * 512 + np.arange(512)[None, :]
            dist = q - (16 * n + 31)
            ok = (dist >= 0) & (n < 255)
            b = rel_bucket_np(dist)
            for h in range(8):
                cmp_t[h, nt, c] = np.where(ok, rb[b, 4 + h], np.float32(NEG))
    return st_a, st_bs, st_bw, st_c, cmp_t


def host_consts():
    ov = np.zeros((128, 2, 65), np.float32)
    for nt in range(2):
        for p in range(128):
            n = nt * 128 + p
            if n >= 255:
                continue
            ov[p, nt, 64] = 1.0
            for m in range(64):
                if (16 * n < 64 * m + 64) and (16 * n + 32 > 64 * m):
                    ov[p, nt, m] = 1.0
    expand = np.zeros((64, 32, 128), np.float32)
    for kt in range(32):
        for kq in range(128):
            expand[2 * kt + kq // 64, kt, kq] = 1.0
    keep = np.ones((128, 32, 64), np.float32)
    add = np.zeros((128, 32, 64), np.float32)
    for qt in range(32):
        for p in range(128):
            cur = (qt * 128 + p) // 64
            for m in range(64):
                forced = (m == 0) or (m == cur) or (m == cur - 1)
                if forced:
                    keep[p, qt, m] = 0.0
                    add[p, qt, m] = 1e6 + 64.0 * m
                elif m > cur:
                    keep[p, qt, m] = 0.0
                    add[p, qt, m] = -1e6 - 64.0 * m
    ident = np.eye(128, dtype=np.float32)
    return ov, expand, keep, add, ident


def build_program(NS, L, debug=False, phases=(1, 2, 3)):
    nc = bass.Bass("TRN2", target_bir_lowering=False)
    es = contextlib.ExitStack()
    with es:
        _build(nc, es, NS, L, debug, phases)
    return nc


def _build(nc, es, NS, L, debug, phases):
    def din(name, shape, dt=F32):
        return nc.dram_tensor(name, list(shape), dt, kind="ExternalInput").ap()

    def dscr(name, shape, dt=BF16, out=False):
        return nc.dram_tensor(name, list(shape), dt, kind=("ExternalOutput" if out else "Internal")).ap()

    k = K(nc, es, arena_slots=186, n_static_f32=2400)
    xT = din("xT", [NS, D, S])
    pT = din("pT", [L, NS, PLE, S])
    i_ng = din("ng", [128, L * 4 * 8])
    i_fn = din("fng", [128, 8])
    i_wi = [din("ffn1_wi", [L, D, 2 * FF]), din("ffn2_wi", [L, D, 2 * FF])]
    i_wo = [din("ffn1_wo", [L, FF, D]), din("ffn2_wo", [L, FF, D])]
    i_win = din("w_in", [L, D, DIN])
    i_dlam = din("dlam", [128, L * 4 * 64])
    i_subln = din("subln", [128, L])
    i_pos = din("posT", [128, L * 2 * 32])
    i_w1 = din("cmp_w1", [L, 2, 2048, 256])
    i_w2 = din("cmp_w2", [L, 2, 256, 64])
    i_sink = din("sinks", [128, L * 8])
    i_wbr = din("w_branch", [L, 3, 512, D])
    i_wout = din("w_out", [L, D, D])
    i_wple = din("w_ple", [L, PLE, D])
    i_wpg = din("w_ple_gate", [L, D, D])
    i_rb31 = din("rb31", [128, 20])
    i_sta = din("st_a", [4, 128, 1280])
    i_stbs = din("st_bs", [8, 128, 1280])
    i_stbw = din("st_bw", [8, 128, 1408])
    i_stc = din("st_c", [8, 128, 1024])
    i_cmpt = din("cmp_t", [8, 2, NCH, 128, 512])
    i_ov = din("ov", [128, 2, 65])
    i_expand = din("expand", [64, 32, 128])
    i_keep = din("keep", [128, 32, 64])
    i_add = din("addc", [128, 32, 64])
    i_ident = din("ident", [128, 128])
    yT = dscr("yT", [NS, D, S], F32, out=True)

    hT = dscr("hT", [NS, D, S], F32)
    b_wi = [dscr("b_wi%d" % i, [L, D, 2 * FF]) for i in range(2)]
    b_wo = [dscr("b_wo%d" % i, [L, FF, D]) for i in range(2)]
    b_win = dscr("b_win", [L, D, DIN])
    b_w1 = dscr("b_w1", [L, 2, 2048, 256])
    b_w2 = dscr("b_w2", [L, 2, 256, 64])
    b_wbr = dscr("b_wbr", [L, 3, 512, D])
    b_wout = dscr("b_wout", [L, D, D])
    b_wple = dscr("b_wple", [L, PLE, D])
    b_wpg = dscr("b_wpg", [L, D, D])
    b_sta = dscr("b_sta", [4, 128, 1280])
    b_stbs = dscr("b_stbs", [8, 128, 1280])
    b_stbw = dscr("b_stbw", [8, 128, 1408])
    b_stc = dscr("b_stc", [8, 128, 1024])
    b_cmpt = dscr("b_cmpt", [8, 2, NCH, 128, 512])
    NFB = 46
    zf = dscr("zf", [NFB, 128, S], BF16, out=debug)
    zav = dscr("zav", [S, 512], BF16, out=debug)
    zv3 = dscr("zv3", [3, S, 128], BF16, out=debug)
    yaT = dscr("yaT", [4, 128, S], BF16, out=debug)
    ybT = dscr("ybT", [3, 4, 128, S], BF16, out=debug)
    ycT = dscr("ycT", [4, 128, S], BF16, out=debug)
    dbg_h = dscr("dbg_h", [3, D, S], F32, out=True) if debug else None
    dbg_sel = dscr("dbg_sel", [2, 64, S], BF16, out=True) if debug else None
    dbg_imp = dscr("dbg_imp", [2, 128, 32 * 64], F32, out=True) if debug else None

    def DR(ap, *keys):
        return T(ap, [k.dres(x) for x in keys])

    ALLCH = list(range(NCH))

    ones16 = k.salloc(128, BF16)
    ident16 = k.salloc(128, BF16)
    ng_sb = k.salloc(L * 4 * 8, F32)
    fn_sb = k.salloc(8, F32)
    rb31 = k.salloc(20, F32)
    subln_sb = k.salloc(L, F32)
    sink_sb = k.salloc(L * 8, F32)
    esink = k.salloc(L * 8, F32)
    lamt = k.salloc(L * 4 * 64, F32)
    neglam = k.salloc(L, F32)
    ov16 = k.salloc(2 * 65, BF16)
    pos16 = k.salloc(L * 2 * 32, BF16)
    tiny = k.salloc(8, F32)

    k.memset(ones16, 1.0)
    k.memset(tiny, 0.0)
    k.dma("pool", ident16, DR(i_ident, "in"))
    k.dma("sp", ng_sb, DR(i_ng, "in"))
    k.dma("sp", fn_sb, DR(i_fn, "in"))
    k.dma("sp", rb31, DR(i_rb31, "in"))
    k.dma("sp", subln_sb, DR(i_subln, "in"))
    k.dma("sp", sink_sb, DR(i_sink, "in"))
    k.dma("sp", lamt, DR(i_dlam, "in"))
    k.dma("pool", ov16, DR(i_ov.rearrange("p a b -> p (a b)"), "in"))
    k.dma("pool", pos16, DR(i_pos, "in"))
    k.act(esink, sink_sb, AF.Exp)
    lam_inits = [0.8 - 0.6 * math.exp(-0.3 * i) for i in range(L)]
    m0 = k.mark()
    tmpl = k.alloc(64, F32)
    tmps = k.alloc(8, F32)
    for l in range(L):
        for j in range(2):
            a = lamt[:, (l * 4 + 2 * j) * 64:(l * 4 + 2 * j + 1) * 64]
            b = lamt[:, (l * 4 + 2 * j + 1) * 64:(l * 4 + 2 * j + 2) * 64]
            k.tt(tmpl, a, b, ALU.mult)
            k.reduce_sum(tmps[:, j:j + 1], tmpl)
        k.act(tmps[:, 2:4], tmps[:, 0:2], AF.Exp)
        k.stt(neglam[:, l:l + 1], tmps[:, 3:4], -lam_inits[l], tmps[:, 2:3], ALU.add, ALU.subtract)
    k.release(m0)

    for src, dst, n in ((i_sta, b_sta, 4), (i_stbs, b_stbs, 8), (i_stbw, b_stbw, 8), (i_stc, b_stc, 8)):
        for h in range(n):
            k.dma("pool", DR(dst[h], "tab"), DR(src[h], "in"))
    for h in range(8):
        for nt in range(2):
            k.dma("pool", DR(b_cmpt[h, nt].rearrange("c p q -> p c q"), "tab"), DR(i_cmpt[h, nt].rearrange("c p q -> p c q"), "in"))

    def cast_weights(l):
        wk = ("w", l)
        for i in range(2):
            for r in range(8):
                k.dma("pool", DR(b_wi[i][l, r * 128:(r + 1) * 128, :], wk), DR(i_wi[i][l, r * 128:(r + 1) * 128, :], "in"))
            for r in range(22):
                k.dma("pool", DR(b_wo[i][l, r * 128:(r + 1) * 128, :], wk), DR(i_wo[i][l, r * 128:(r + 1) * 128, :], "in"))
        for r in range(8):
            k.dma("pool", DR(b_win[l, r * 128:(r + 1) * 128, :], wk), DR(i_win[l, r * 128:(r + 1) * 128, :], "in"))
            k.dma("pool", DR(b_wout[l, r * 128:(r + 1) * 128, :], wk), DR(i_wout[l, r * 128:(r + 1) * 128, :], "in"))
            k.dma("pool", DR(b_wpg[l, r * 128:(r + 1) * 128, :], wk), DR(i_wpg[l, r * 128:(r + 1) * 128, :], "in"))
        for m in range(3):
            for r in range(4):
                k.dma("pool", DR(b_wbr[l, m, r * 128:(r + 1) * 128, :], wk), DR(i_wbr[l, m, r * 128:(r + 1) * 128, :], "in"))
        for r in range(2):
            k.dma("pool", DR(b_wple[l, r * 128:(r + 1) * 128, :], wk), DR(i_wple[l, r * 128:(r + 1) * 128, :], "in"))
        for kv in range(2):
            for r in range(16):
                k.dma("pool", DR(b_w1[l, kv, r * 128:(r + 1) * 128, :], wk), DR(i_w1[l, kv, r * 128:(r + 1) * 128, :], "in"))
            for r in range(2):
                k.dma("pool", DR(b_w2[l, kv, r * 128:(r + 1) * 128, :], wk), DR(i_w2[l, kv, r * 128:(r + 1) * 128, :], "in"))

    def rmsnorm(h8, gcol, nT, sq, rstd):
        for kc in range(8):
            k.act(sq[kc], h8[kc], AF.Square)
        ps = k.next_s()
        for kc in range(8):
            k.mm(ps, ones16, sq[kc], start=(kc == 0), stop=(kc == 7))
        k.act(rstd, ps, AF.Sqrt, bias=EPS, scale=1.0 / D)
        k.recip(rstd, rstd)
        for kc in range(8):
            k.stt(nT[kc], h8[kc], gcol(kc), rstd, ALU.mult, ALU.mult)

    def ffn(l, which, h8, nT, tl):
        wi = b_wi[which][l].rearrange("(kc p) c -> p kc c", p=128)
        wo = b_wo[which][l].rearrange("(j p) c -> p j c", p=128)
        wk = ("w", l)
        actT = tl["act"]
        for jb in range(11):
            wg = tl["wi"][(2 * jb) % 4]
            wu = tl["wi"][(2 * jb + 1) % 4]
            k.dma("sp", T(wg.ap.rearrange("p (a b) -> p a b", a=8), wg.res), DR(wi[:, :, jb * 256:(jb + 1) * 256], wk))
            k.dma("sp", T(wu.ap.rearrange("p (a b) -> p a b", a=8), wu.res), DR(wi[:, :, FF + jb * 256:FF + (jb + 1) * 256], wk))
            for jj in range(2):
                j = 2 * jb + jj
                pg = k.banks[4 + (2 * j) % 4]
                pu = k.banks[4 + (2 * j + 1) % 4]
                for kc in range(8):
                    k.mm(pg, wg[:, kc * 256 + jj * 128: kc * 256 + jj * 128 + 128], nT[kc], start=(kc == 0), stop=(kc == 7))
                for kc in range(8):
                    k.mm(pu, wu[:, kc * 256 + jj * 128: kc * 256 + jj * 128 + 128], nT[kc], start=(kc == 0), stop=(kc == 7))
                sg = tl["sg"][j % 2]
                k.act(sg, pg, AF.Silu)
                k.tt(actT[j], sg, pu, ALU.mult)
        for qd in range(4):
            wt = tl["wo"][qd % 2]
            k.dma("sp", T(wt.ap.rearrange("p (a b) -> p a b", a=22), wt.res), DR(wo[:, :, qd * 256:(qd + 1) * 256], wk))
            for mm_ in range(2):
                m = qd * 2 + mm_
                po = k.next_s()
                for j in range(22):
                    k.mm(po, wt[:, j * 256 + mm_ * 128: j * 256 + mm_ * 128 + 128], actT[j], start=(j == 0), stop=(j == 21))
                k.stt(h8[m], po, 0.5, h8[m], ALU.mult, ALU.add)

    def dense_tiles():
        tl = {}
        tl["h8"] = [[k.alloc(512, F32) for _ in range(8)] for _ in range(2)]
        tl["sq"] = [k.alloc(512, BF16) for _ in range(8)]
        tl["nT"] = [[k.alloc(512, BF16) for _ in range(8)] for _ in range(2)]
        tl["rstd"] = [k.alloc(512, F32) for _ in range(2)]
        tl["act"] = [k.alloc(512, BF16) for _ in range(22)]
        tl["wi"] = [k.alloc(8 * 256, BF16) for _ in range(4)]
        tl["wo"] = [k.alloc(22 * 256, BF16) for _ in range(2)]
        tl["sg"] = [k.alloc(512, F32) for _ in range(2)]
        return tl

    FB = []
    for h in range(4):
        FB.append((h, [(C_AQ + 128 * h, 128)], "q"))
    for h in range(4):
        FB.append((4 + h, [(C_AK + 128 * h, 128)], "c"))
    for b in range(4):
        FB.append((8 + b, [(C_BQ + 64 * b, 64), (C_BQ + 64 * (b + 4), 64)], "q"))
    FB.append((12, [(C_BKC, 128)], "c"))
    FB.append((13, [(C_BVC, 128)], "c"))
    FB.append((14, [(C_BKS, 128)], "c"))
    FB.append((15, [(C_BKW, 128)], "c"))
    FB.append((16, [(C_BG, 24)], "s"))
    for b in range(4):
        FB.append((17 + b, [(C_CQ + 64 * b, 64), (C_CQ + 64 * (b + 4), 64)], "q"))
    FB.append((21, [(C_CK, 128)], "c"))
    for i in range(8):
        FB.append((22 + i, [(C_GA + 128 * i, 128)], "s"))
    for i in range(8):
        FB.append((30 + i, [(C_GB + 128 * i, 128)], "s"))
    for i in range(8):
        FB.append((38 + i, [(C_GC + 128 * i, 128)], "s"))

    def phase1(l, s):
        m0 = k.mark()
        tl = dense_tiles()
        wt_in = [k.alloc(8 * 128, BF16) for _ in range(3)]
        wt_av = [k.alloc(8 * 512, BF16) for _ in range(1)]
        ev = [k.alloc(512, BF16) for _ in range(4)]
        win = b_win[l].rearrange("(kc p) c -> p kc c", p=128)
        wk = ("w", l)
        src = xT if l == 0 else hT
        import os
        STAGE = int(os.environ.get("STAGE", "9"))
        for c in range(int(os.environ.get("NCHT", NCH))):
            h8 = tl["h8"][c % 2]
            nT = tl["nT"][c % 2]
            rstd = tl["rstd"][c % 2]
            tok = slice(c * TC, (c + 1) * TC)
            hkey = ("h", s, c)
            for kc in range(8):
                k.dma("sp", h8[kc], DR(src[s, kc * 128:(kc + 1) * 128, tok], hkey if l > 0 else "in"))
            rmsnorm(h8, lambda kc: ng_sb[:, (l * 4 + 0) * 8 + kc:(l * 4 + 0) * 8 + kc + 1], nT, tl["sq"], rstd)
            if STAGE >= 2:
                ffn(l, 0, h8, nT, tl)
            for kc in range(8):
                k.dma("pool", DR(hT[s, kc * 128:(kc + 1) * 128, tok], hkey), h8[kc])
                if debug and l == 0 and s == 0:
                    k.dma("pool", DR(dbg_h[0, kc * 128:(kc + 1) * 128, tok], "dbg"), h8[kc], is_out=True)
            if STAGE < 3:
                continue
            nT2 = tl["nT"][(c + 1) % 2]
            rmsnorm(h8, lambda kc: ng_sb[:, (l * 4 + 1) * 8 + kc:(l * 4 + 1) * 8 + kc + 1], nT2, tl["sq"], rstd)
            zkey = ("z", c)
            for bi, (zb, cols, kind) in enumerate(FB):
                wt = wt_in[bi % 3]
                w3 = T(wt.ap.rearrange("p (a b) -> p a b", a=8), wt.res)
                off = 0
                for (c0, ncol) in cols:
                    k.dma("sp", w3[:, :, off:off + ncol], DR(win[:, :, c0:c0 + ncol], wk))
                    off += ncol
                M = off
                ps = k.next_s()
                for kc in range(8):
                    k.mm(ps[0:M, :], wt[:, kc * 128: kc * 128 + M], nT2[kc], start=(kc == 0), stop=(kc == 7))
                e = ev[bi % 4]
                if kind == "q":
                    k.act(e[0:M, :], ps[0:M, :], AF.Copy, scale=0.125) if bi % 2 == 0 else k.ts(e[0:M, :], ps[0:M, :], 0.125, None, ALU.mult)
                elif kind == "c":
                    k.act(e[0:M, :], ps[0:M, :], AF.Copy) if bi % 2 == 0 else k.copy(e[0:M, :], ps[0:M, :])
                else:
                    k.act(e[0:M, :], ps[0:M, :], AF.Sigmoid)
                k.dma("pool", DR(zf[zb, 0:M, tok], zkey), e[0:M, :])
            if STAGE < 4:
                continue
            wav = wt_av[0]
            k.dma("sp", T(wav.ap.rearrange("p (a b) -> p a b", a=8), wav.res), DR(win[:, :, C_AV:C_AV + 512], wk))
            for tt_ in range(4):
                ps = k.next_s()
                for kc in range(8):
                    k.mm(ps, nT2[kc][:, tt_ * 128:(tt_ + 1) * 128], wav[:, kc * 512:(kc + 1) * 512], start=(kc == 0), stop=(kc == 7))
                e = ev[tt_ % 4]
                k.copy(e, ps)
                k.dma("pool", DR(zav[c * TC + tt_ * 128: c * TC + (tt_ + 1) * 128, :], zkey), e)
            wv = wt_in[0]
            w3 = T(wv.ap.rearrange("p (a b) -> p a b", a=8), wv.res)
            for vi, c0 in enumerate((C_BVS, C_BVW, C_CV)):
                wv = wt_in[vi % 3]
                w3 = T(wv.ap.rearrange("p (a b) -> p a b", a=8), wv.res)
                k.dma("sp", w3, DR(win[:, :, c0:c0 + 128], wk))
                ps = k.next_s()
                for tt_ in range(4):
                    for kc in range(8):
                        k.mm(ps[:, tt_ * 128:(tt_ + 1) * 128], nT2[kc][:, tt_ * 128:(tt_ + 1) * 128], wv[:, kc * 128:(kc + 1) * 128],
                             start=(kc == 0), stop=(kc == 7))
                e = ev[vi % 4]
                k.act(e, ps, AF.Copy)
                k.dma("pool", DR(zv3[vi, c * TC:(c + 1) * TC, :].rearrange("(t p) e -> p t e", p=128), zkey),
                      T(e.ap.rearrange("p (t e) -> p t e", t=4), e.res))
        k.release(m0)

    def run_attn(steps, Pb):
        n = len(steps)
        deferred = []
        for i in range(n + 2):
            due = [d for d in deferred if d[0] <= i]
            deferred = [d for d in deferred if d[0] > i]
            for d in due:
                d[1]()
            if i < n:
                st = steps[i]
                Sb = k.next_s()
                nk = st["nk"]
                nq = len(st["qk"])
                for qi, (lh, rh) in enumerate(st["qk"]):
                    k.mm(Sb[0:nk, :], lh, rh, start=(qi == 0), stop=(qi == nq - 1))
                P = Pb[i % 4]
                if st["bias"] is not None and os.environ.get("A_NOBIAS", "0") == "0":
                    k.act(P[0:nk, :], Sb[0:nk, :], AF.Exp, bias=st["bias"][0:nk, :])
                else:
                    k.act(P[0:nk, :], Sb[0:nk, :], AF.Exp)
            j = i - 2
            if j >= 0:
                st = steps[j]
                P = Pb[j % 4]
                nk = st["nk"]
                for (o, lh, sta, sto) in st["pv"]:
                    k.mm(o, lh, P[0:nk, :], start=sta, stop=sto)
                if st.get("pv2"):
                    st["pv2"](P, Pb[(j - 1) % 4])
                if st.get("after"):
                    cont = st["after"]()
                    if cont is not None:
                        deferred.append((i + 4, cont))
        for d in deferred:
            d[1]()

    def phase2(l, s):
        m0 = k.mark()
        Pb = [k.alloc(512, BF16) for _ in range(4)]
        qT = [k.alloc(S, BF16) for _ in range(2)]
        kT = [k.alloc(S, BF16) for _ in range(2)]
        Vt = [k.alloc(32 * 128, BF16) for _ in range(2)]
        stp = [k.alloc(1280, BF16) for _ in range(2)]
        t0 = [k.alloc(512, F32) for _ in range(2)]
        rr = [k.alloc(512, F32) for _ in range(2)]
        osq = k.alloc(512, BF16)
        rs2 = k.alloc(512, F32)
        yo = [k.alloc(512, BF16) for _ in range(2)]
        zall = [("z", c) for c in ALLCH]
        job = 0
        import os
        P2 = os.environ.get("P2", "ACWMS")
        for h in (range(4) if "A" in P2 else []):
            q_, k_, v_, st_ = qT[h % 2], kT[h % 2], Vt[h % 2], stp[h % 2]
            k.dma("sp", q_, DR(zf[h], *zall))
            k.dma("sp", k_, DR(zf[4 + h], *zall))
            v3_ = T(v_.ap.rearrange("p (t e) -> p t e", t=32), v_.res)
            for t8 in range(4):
                k.dma("sp", v3_[:, t8 * 8:(t8 + 1) * 8, :],
                      DR(zav[t8 * 1024:(t8 + 1) * 1024, h * 128:(h + 1) * 128].rearrange("(t p) e -> p t e", p=128), *zall))
            k.dma("sp", st_, DR(b_sta[h], "tab"))
            steps = []
            for c in range(int(os.environ.get("A_NC", NCH))):
                for mp in range(2):
                    ob = k.banks[4 + (job % 2) * 2]
                    sb_ = k.banks[5 + (job % 2) * 2]
                    job += 1
                    kts = list(range(0, 4 * c + 4))
                    for kt in kts:
                        qk = [(k_[mp * 64:(mp + 1) * 64, kt * 128:(kt + 1) * 128], q_[mp * 64:(mp + 1) * 64, c * TC:(c + 1) * TC])]
                        bias = None
                        if kt >= 4 * c - 1:
                            off = (512 * c - 128 * kt) + 384
                            qk.append((ident16, st_[:, off:off + 512]))
                        else:
                            qk.append((ident16, st_[:, 768:1280]))
                        first, last = (kt == kts[0]), (kt == kts[-1])
                        stp_ = dict(qk=qk, nk=128, bias=bias,
                                    pv=[(ob, v_[:, kt * 128:(kt + 1) * 128], first, last), (sb_, ones16, first, last)])
                        if last and os.environ.get("A_AFTER", "1") == "1":
                            def after(ob=ob, sb_=sb_, mp=mp, c=c, h=h):
                                r = rr[mp]
                                k.recip(r, sb_)
                                if mp == 0:
                                    k.tt(t0[0], ob, r, ALU.mult)
                                else:
                                    k.tt(t0[1], ob, r, ALU.mult)
                                    k.stt(t0[0], t0[1], neglam[:, l:l + 1], t0[0], ALU.mult, ALU.add)
                                    k.act(osq, t0[0], AF.Square)

                                    def cont(c=c, h=h):
                                        ps = k.next_s()
                                        k.mm(ps, ones16, osq)
                                        k.act(rs2, ps, AF.Sqrt, bias=EPS, scale=1.0 / 128)
                                        k.recip(rs2, rs2)
                                        k.ts(rs2, rs2, 1.0 - lam_inits[l], None, ALU.mult)
                                        y = yo[c % 2]
                                        k.stt(y, t0[0], subln_sb[:, l:l + 1], rs2, ALU.mult, ALU.mult)
                                        k.dma("pool", DR(yaT[h, :, c * TC:(c + 1) * TC], ("ya", c)), y)
                                    return cont
                                return None
                            stp_["after"] = after
                        steps.append(stp_)
            run_attn(steps, Pb)
        k.release(m0)


        m0 = k.mark()
        Pb = [k.alloc(512, BF16) for _ in range(4)]
        qT = [k.alloc(S, BF16) for _ in range(2)]
        kTb = k.alloc(S, BF16)
        Va = k.alloc(32 * 2 * 65 + 64, BF16)
        Va4 = T(Va.ap[:, 0:32 * 2 * 65].rearrange("p (t g e) -> p t g e", t=32, g=2), Va.res)
        stp = [k.alloc(1408, BF16) for _ in range(2)]
        rrow = [k.alloc(512, F32) for _ in range(4)]
        bc1 = [k.alloc(512, F32, parts=64) for _ in range(4)]
        bc2 = [k.alloc(512, BF16, parts=64) for _ in range(4)]
        t1 = [k.alloc(512, F32) for _ in range(2)]
        yo = [k.alloc(512, BF16) for _ in range(4)]
        ones32 = k.alloc(64, F32)
        k.memset(ones32, 1.0)
        jobc = [0]

        def load_v(vi):
            k.memset(Va, 1.0)
            for g in range(2):
                for t8 in range(4):
                    k.dma("sp", Va4[:, t8 * 8:(t8 + 1) * 8, g, 0:64],
                          DR(zv3[vi, t8 * 1024:(t8 + 1) * 1024, g * 64:(g + 1) * 64].rearrange("(t p) e -> p t e", p=128), *zall))

        def finalize(accb, base, hh, c, dst, dkey, gate_row=None, sink=None, defer=False):
            i = jobc[0] % 4
            jobc[0] += 1
            r = rrow[i]
            if sink is not None:
                k.ts(r[64:65, :], accb[64:65, :], sink, None, ALU.add)
            else:
                k.ts(r[64:65, :], accb[64:65, :], 1e-30, None, ALU.max)
            k.recip(r[64:65, :], r[64:65, :])
            if gate_row is not None:
                k.dma("sp", bc2[i], DR(zf[16, gate_row:gate_row + 1, c * TC:(c + 1) * TC].partition_broadcast(64), ("z", c)))

            def cont():
                pb = k.next_s()
                k.mm(pb[0:64, :], ones32[64:65, 0:64], r[64:65, :])
                k.act(bc1[i], pb[0:64, :], AF.Copy)
                y = yo[i]
                if gate_row is not None:
                    k.tt(t1[i % 2][0:64, :], accb[0:64, :], bc1[i], ALU.mult)
                    k.tt(y[0:64, :], t1[i % 2][0:64, :], bc2[i], ALU.mult)
                else:
                    k.tt(y[0:64, :], accb[0:64, :], bc1[i], ALU.mult)
                k.dma("pool", DR(dst[base:base + 64, c * TC:(c + 1) * TC], dkey), y[0:64, :])
            if defer:
                return cont
            cont()
            return None

        accn = [0]

        def gqa(qblk0, kblk, vi, strips, kts_fn, diag_fn, dst, dkeyf, sinks=False, gate_j=None, bias0=0, extra_fn=None):
            k.dma("sp", kTb, DR(zf[kblk], *zall))
            load_v(vi)
            ncol = strips.shape[-1]
            for b in range(4):
                q_ = qT[b % 2]
                k.dma("sp", q_, DR(zf[qblk0 + b], *zall))
                for half in range(2):
                    hh = b + 4 * half
                    base = 64 * half
                    st_ = stp[hh % 2]
                    k.dma("sp", st_[:, 0:ncol], DR(strips[hh], "tab"))
                    steps = []
                    for c in range(NCH):
                        acc = k.banks[4 + accn[0] % 4]
                        accn[0] += 1
                        kts = kts_fn(c)
                        for kt in kts:
                            qk = [(kTb[base:base + 64, kt * 128:(kt + 1) * 128], q_[base:base + 64, c * TC:(c + 1) * TC])]
                            if extra_fn is not None:
                                qk.append(extra_fn(kt, c, half))
                            bias = None
                            if diag_fn(kt, c):
                                off = (512 * c - 128 * kt) + 384
                                qk.append((ident16, st_[:, off:off + 512]))
                            else:
                                qk.append((ident16, st_[:, 768:1280]))
                            first, last = (kt == kts[0]), (kt == kts[-1])
                            vo_ = (kt * 2 + half) * 65
                            stp_ = dict(qk=qk, nk=128, bias=bias, pv=[(acc, Va[:, vo_:vo_ + 128], first, last)])
                            if last:
                                def after(acc=acc, base=base, hh=hh, c=c, b=b):
                                    return finalize(acc, base, hh, c, dst[b], dkeyf(c),
                                                    gate_row=(hh * 3 + gate_j) if gate_j is not None else None,
                                                    sink=(esink[64:65, l * 8 + hh: l * 8 + hh + 1] if sinks else None), defer=True)
                                stp_["after"] = after
                            steps.append(stp_)
                    run_attn(steps, Pb)

        if "C" in P2:
            gqa(17, 21, 2, b_stc, lambda c: [kt for kt in range(4 * c - 1, 4 * c + 4) if kt >= 0], lambda kt, c: True,
                ycT, lambda c: ("yc", c), sinks=True)

        if "W" in P2:
            gqa(8, 15, 1, b_stbw, lambda c: [kt for kt in range(4 * c - 4, 4 * c + 4) if kt >= 0], lambda kt, c: True,
                ybT[2], lambda c: ("yb", 2, c), gate_j=2)

        wk = ("w", l)
        w1sb = k.alloc(32 * 256, BF16)
        w1v = T(w1sb.ap.rearrange("p (a b) -> p a b", a=32), w1sb.res)
        w2p = k.alloc(2 * 2 * 128, BF16)
        gel = [[k.alloc(256, BF16) for _ in range(2)] for _ in range(2)]
        xg = k.alloc(256, F32)
        ug = k.alloc(256, F32)
        cvec = k.alloc(8, F32)
        kcmpT = k.alloc(256, BF16)
        vcmp = k.alloc(2 * 2 * 65, BF16)
        ctab = [k.alloc(NCH * 512, BF16) for _ in range(2)]
        impsum = [k.alloc(32 * 64, F32) for _ in range(2)]
        keepA = k.alloc(32 * 64, F32)
        addA = k.alloc(32 * 64, F32)
        selT = [k.alloc(S, BF16) for _ in range(2)]
        expand16 = k.alloc(32 * 128, BF16)
        k.memset(expand16, 0.0)
        k.memset(selT[0], 0.0)
        k.memset(selT[1], 0.0)
        vsel = [k.alloc(64, F32) for _ in range(2)]
        v2sel = k.alloc(64, F32)
        m8a = k.alloc(8, F32)
        m8b = k.alloc(8, F32)
        mneg = [k.alloc(64, BF16) for _ in range(2)]
        recq = k.alloc(8, F32)
        k.dma("pool", expand16[0:64, :], DR(i_expand.rearrange("p a b -> p (a b)"), "in"))
        k.dma("sp", keepA, DR(i_keep.rearrange("p a b -> p (a b)"), "in"))
        k.dma("sp", addA, DR(i_add.rearrange("p a b -> p (a b)"), "in"))
        GC = 2.0 * math.sqrt(2.0 / math.pi)
        k.memset(vcmp, 1.0)
        for kv in (range(2) if "M" in P2 else []):
            for half in range(2):
                for a8 in range(8):
                    k.dma("sp", w1v[half * 64:(half + 1) * 64, a8 * 4:(a8 + 1) * 4, :],
                          DR(b_w1[l, kv, a8 * 256:(a8 + 1) * 256, :].rearrange("(a d) j -> d a j", d=64), wk))
            k.memset(w2p, 0.0)
            for g in range(2):
                for jc in range(2):
                    o_ = (jc * 2 + g) * 128 + g * 64
                    k.dma("sp", w2p[:, o_:o_ + 64], DR(b_w2[l, kv, jc * 128:(jc + 1) * 128, :], wk))
            k.dma("sp", kTb, DR(zf[12 + kv], *zall))
            for jc in range(2):
                ps = k.next_s()
                for a in range(32):
                    pc = (l * 2 + kv) * 32 + a
                    k.mm(ps[:, 0:1], w1sb[0:64, a * 256 + jc * 128: a * 256 + jc * 128 + 128], pos16[0:64, pc:pc + 1],
                         start=(a == 0), stop=(a == 31))
                k.copy(cvec[:, kv * 2 + jc: kv * 2 + jc + 1], ps[:, 0:1])
            for g in range(2):
                for jc in range(2):
                    ps = k.next_s()
                    for a in range(32):
                        k.mm(ps[:, 0:255], w1sb[g * 64:(g + 1) * 64, a * 256 + jc * 128: a * 256 + jc * 128 + 128],
                             kTb[g * 64:(g + 1) * 64, a: a + 16 * 254 + 1: 16], start=(a == 0), stop=(a == 31))
                    k.ts(xg[:, 0:255], ps[:, 0:255], cvec[:, kv * 2 + jc: kv * 2 + jc + 1], None, ALU.add)
                    k.tt(ug[:, 0:255], xg[:, 0:255], xg[:, 0:255], ALU.mult)
                    k.ts(ug[:, 0:255], ug[:, 0:255], 0.044715, 1.0, ALU.mult, ALU.add)
                    k.tt(ug[:, 0:255], ug[:, 0:255], xg[:, 0:255], ALU.mult)
                    k.act(ug[:, 0:255], ug[:, 0:255], AF.Sigmoid, scale=GC)
                    k.tt(gel[g][jc][:, 0:255], ug[:, 0:255], xg[:, 0:255], ALU.mult)
            if kv == 0:
                ps = k.next_s()
                n_ = 0
                for g in range(2):
                    for jc in range(2):
                        o_ = (jc * 2 + g) * 128
                        k.mm(ps[:, 0:255], w2p[:, o_:o_ + 128], gel[g][jc][:, 0:255], start=(n_ == 0), stop=(n_ == 3))
                        n_ += 1
                k.copy(kcmpT[:, 0:255], ps[:, 0:255])
            else:
                for nt in range(2):
                    nk = 128 if nt == 0 else 127
                    for g in range(2):
                        ps = k.next_s()
                        for jc in range(2):
                            o_ = (jc * 2 + 0) * 128
                            k.mm(ps[0:nk, 0:64], gel[g][jc][:, nt * 128: nt * 128 + nk], w2p[:, o_:o_ + 64], start=(jc == 0), stop=(jc == 1))
                        o2 = (nt * 2 + g) * 65
                        k.copy(vcmp[0:nk, o2:o2 + 64], ps[0:nk, 0:64])

        jobn = [0]
        for b in (range(4) if "M" in P2 else []):
            q_ = qT[b % 2]
            k.dma("sp", q_, DR(zf[8 + b], *zall))
            for half in range(2):
                hh = b + 4 * half
                g = half
                base = 64 * half
                for nt in range(2):
                    k.dma("sp", T(ctab[nt].ap.rearrange("p (c q) -> p c q", c=NCH), ctab[nt].res),
                          DR(b_cmpt[hh, nt].rearrange("c p q -> p c q"), "tab"))
                steps = []
                for c in range(NCH):
                    acc = k.banks[4 + (jobn[0] % 2) * 2]
                    impb = k.banks[5 + (jobn[0] % 2) * 2]
                    jobn[0] += 1
                    nts = [0] if c < 4 else [0, 1]
                    for nt in nts:
                        nk = 128 if nt == 0 else 127
                        first, last = (nt == nts[0]), (nt == nts[-1])
                        qk = [(kcmpT[base:base + 64, nt * 128: nt * 128 + nk], q_[base:base + 64, c * TC:(c + 1) * TC]),
                              (ident16[:, 0:nk], ctab[nt][:, c * 512:(c + 1) * 512])]
                        o2 = (nt * 2 + g) * 65

                        def pv2(P, Pprev, nk=nk, nt=nt, impb=impb, first=first, last=last):
                            if not last:
                                return
                            for qt in range(4):
                                reg = impb[:, qt * 65:(qt + 1) * 65]
                                if first:
                                    k.mm(reg, P[0:nk, qt * 128:(qt + 1) * 128], ov16[0:nk, nt * 65:(nt + 1) * 65], start=True, stop=True)
                                else:
                                    k.mm(reg, Pprev[0:128, qt * 128:(qt + 1) * 128], ov16[0:128, 0:65], start=True, stop=False)
                                    k.mm(reg, P[0:nk, qt * 128:(qt + 1) * 128], ov16[0:nk, nt * 65:(nt + 1) * 65], start=False, stop=True)
                        stp_ = dict(qk=qk, nk=nk, bias=None, pv=[(acc[0:65, :], vcmp[0:nk, o2:o2 + 65], first, last)], pv2=pv2)
                        if last:
                            def after(acc=acc, impb=impb, base=base, hh=hh, c=c, b=b, g=g):
                                finalize(acc, base, hh, c, ybT[0][b], ("yb", 0, c), gate_row=hh * 3 + 0)
                                for qt in range(4):
                                    k.ts(recq[:, qt:qt + 1], impb[:, qt * 65 + 64: qt * 65 + 65], 1e-30, None, ALU.max)
                                k.recip(recq[:, 0:4], recq[:, 0:4])
                                for qt in range(4):
                                    d_ = impsum[g][:, (c * 4 + qt) * 64:(c * 4 + qt + 1) * 64]
                                    if b == 0:
                                        k.ts(d_, impb[:, qt * 65: qt * 65 + 64], recq[:, qt:qt + 1], None, ALU.mult)
                                    else:
                                        k.stt(d_, impb[:, qt * 65: qt * 65 + 64], recq[:, qt:qt + 1], d_, ALU.mult, ALU.add)
                            stp_["after"] = after
                        steps.append(stp_)
                run_attn(steps, Pb)

        for g in (range(2) if "M" in P2 else []):
            for qt in range(32):
                v = vsel[qt % 2]
                k.tt(v, impsum[g][:, qt * 64:(qt + 1) * 64], keepA[:, qt * 64:(qt + 1) * 64], ALU.mult)
                k.tt(v, v, addA[:, qt * 64:(qt + 1) * 64], ALU.add)
                k.max8(m8a, v)
                k.match_replace(v2sel, m8a, v, -3e9)
                k.max8(m8b, v2sel)
                k.ts(v2sel, v, m8b[:, 7:8], None, ALU.is_ge)
                mn = mneg[qt % 2]
                k.ts(mn, v2sel, -1.0, -NEG, ALU.add, ALU.mult)
                ps = k.next_s()
                k.mm(ps[0:64, 0:128], mn, ident16)
                k.copy(selT[g][0:64, qt * 128:(qt + 1) * 128], ps[0:64, 0:128])

        if debug and "M" in P2:
            for g in range(2):
                k.dma("pool", DR(dbg_sel[g], "dbg"), selT[g][0:64, :])
                k.dma("pool", DR(dbg_imp[g], "dbg"), impsum[g])
        if "S" in P2:
          gqa(8, 14, 0, b_stbs, lambda c: list(range(0, 4 * c + 4)), lambda kt, c: kt >= 4 * c - 1,
            ybT[1], lambda c: ("yb", 1, c), gate_j=1, bias0=4,
            extra_fn=lambda kt, c, half: (expand16[:, kt * 128:(kt + 1) * 128], selT[half][:, c * TC:(c + 1) * TC]))
        k.release(m0)


    phase2_impl = phase2

    def phase3(l, s, last):
        m0 = k.mark()
        tl = {}
        tl["h8"] = [[k.alloc(512, F32) for _ in range(8)] for _ in range(2)]
        tl["sq"] = [k.alloc(512, BF16) for _ in range(8)]
        tl["nT"] = [[k.alloc(512, BF16) for _ in range(8)] for _ in range(2)]
        tl["rstd"] = [k.alloc(512, F32) for _ in range(2)]
        tl["sg"] = [k.alloc(512, F32) for _ in range(2)]
        m1 = k.mark()
        tl["act"] = [k.alloc(512, BF16) for _ in range(22)]
        tl["wi"] = [k.alloc(8 * 256, BF16) for _ in range(4)]
        tl["wo"] = [k.alloc(22 * 256, BF16) for _ in range(2)]
        m2 = k.mark()
        k.release(m1)
        ytile = [k.alloc(512, BF16) for _ in range(20)]
        gt = [k.alloc(512, BF16) for _ in range(6)]
        wbt = [k.alloc(4 * 128, BF16) for _ in range(6)]
        mg = [k.alloc(512, BF16) for _ in range(8)]
        mtmp = [k.alloc(512, F32) for _ in range(2)]
        wot = [k.alloc(8 * 128, BF16) for _ in range(3)]
        pt = [k.alloc(2 * 512, BF16) for _ in range(1)]
        gsig = [k.alloc(512, F32) for _ in range(2)]
        k.release(max(m2, k.mark()))
        wk = ("w", l)
        wbr = b_wbr[l]
        wout = b_wout[l].rearrange("(kc p) c -> p kc c", p=128)
        wpg = b_wpg[l].rearrange("(kc p) c -> p kc c", p=128)
        wple = b_wple[l].rearrange("(kc p) c -> p kc c", p=128)
        for c in range(NCH):
            h8 = tl["h8"][c % 2]
            tok = slice(c * TC, (c + 1) * TC)
            hkey = ("h", s, c)
            for kc in range(8):
                k.dma("sp", h8[kc], DR(hT[s, kc * 128:(kc + 1) * 128, tok], hkey))
            for i in range(4):
                k.dma("sp", ytile[i], DR(yaT[i, :, tok], ("ya", c)))
                for j in range(3):
                    k.dma("sp", ytile[4 + 4 * j + i], DR(ybT[j, i, :, tok], ("yb", j, c)))
                k.dma("sp", ytile[16 + i], DR(ycT[i, :, tok], ("yc", c)))
            kch = []
            for i in range(4):
                kch.append((i, 0, [(128 * i, 128)]))
            for j in range(3):
                for i in range(4):
                    kch.append((4 + 4 * j + i, 1, [(64 * i, 64), (64 * (i + 4), 64)]))
            for i in range(4):
                kch.append((16 + i, 2, [(64 * i, 64), (64 * (i + 4), 64)]))
            for ob in range(8):
                for m in range(3):
                    k.dma("sp", gt[(ob % 2) * 3 + m], DR(zf[22 + 8 * m + ob, :, tok], ("z", c)))
                wb = [wbt[(ob % 2) * 3 + m] for m in range(3)]
                w3_ = T(wb[0].ap.rearrange("p (a b) -> p a b", a=4), wb[0].res)
                k.dma("sp", w3_, DR(wbr[0, :, ob * 128:(ob + 1) * 128].rearrange("(i p) c -> p i c", p=128), wk))
                for m in (1, 2):
                    w3_ = T(wb[m].ap.rearrange("p (a b) -> p a b", a=4), wb[m].res)
                    for hf in range(2):
                        k.dma("sp", w3_[hf * 64:(hf + 1) * 64, :, :],
                              DR(wbr[m, hf * 256:(hf + 1) * 256, ob * 128:(ob + 1) * 128].rearrange("(i p) c -> p i c", p=64), wk))
                for m in range(3):
                    ps = k.next_s()
                    mine = [x for x in kch if x[1] == m]
                    for xi, (yi, _, rows) in enumerate(mine):
                        i4 = xi % 4
                        k.mm(ps, wb[m][:, i4 * 128:(i4 + 1) * 128], ytile[yi], start=(xi == 0), stop=(xi == len(mine) - 1))
                    g = gt[(ob % 2) * 3 + m]
                    if m == 0:
                        k.tt(mtmp[ob % 2], ps, g, ALU.mult)
                    elif m == 1:
                        k.tt(gsig[ob % 2], ps, g, ALU.mult)
                        k.tt(mtmp[ob % 2], mtmp[ob % 2], gsig[ob % 2], ALU.add)
                    else:
                        k.tt(gsig[ob % 2], ps, g, ALU.mult)
                        k.tt(mg[ob], mtmp[ob % 2], gsig[ob % 2], ALU.add)
            for ob in range(8):
                w = wot[ob % 3]
                k.dma("sp", T(w.ap.rearrange("p (a b) -> p a b", a=8), w.res), DR(wout[:, :, ob * 128:(ob + 1) * 128], wk))
                ps = k.next_s()
                for kc in range(8):
                    k.mm(ps, w[:, kc * 128:(kc + 1) * 128], mg[kc], start=(kc == 0), stop=(kc == 7))
                k.tt(h8[ob], h8[ob], ps, ALU.add)
            if debug and l == 0 and s == 0:
                for kc in range(8):
                    k.dma("pool", DR(dbg_h[1, kc * 128:(kc + 1) * 128, tok], "dbg"), h8[kc], is_out=True)
            nT = tl["nT"][c % 2]
            rstd = tl["rstd"][c % 2]
            rmsnorm(h8, lambda kc: ng_sb[:, (l * 4 + 2) * 8 + kc:(l * 4 + 2) * 8 + kc + 1], nT, tl["sq"], rstd)
            ffn(l, 1, h8, nT, tl)
            if debug and l == 0 and s == 0:
                for kc in range(8):
                    k.dma("pool", DR(dbg_h[2, kc * 128:(kc + 1) * 128, tok], "dbg"), h8[kc], is_out=True)
            nT2 = tl["nT"][(c + 1) % 2]
            rmsnorm(h8, lambda kc: ng_sb[:, (l * 4 + 3) * 8 + kc:(l * 4 + 3) * 8 + kc + 1], nT2, tl["sq"], rstd)
            p_ = pt[0]
            k.dma("pool", T(p_.ap.rearrange("p (a b) -> p a b", a=2), p_.res),
                  DR(pT[l, s, :, tok].rearrange("(a p) t -> p a t", p=128), "in"))
            for ob in range(8):
                w = wot[ob % 3]
                k.dma("sp", T(w.ap.rearrange("p (a b) -> p a b", a=8), w.res), DR(wpg[:, :, ob * 128:(ob + 1) * 128], wk))
                ps = k.next_s()
                for kc in range(8):
                    k.mm(ps, w[:, kc * 128:(kc + 1) * 128], nT2[kc], start=(kc == 0), stop=(kc == 7))
                gs = gsig[ob % 2]
                k.act(gs, ps, AF.Sigmoid)
                w2 = wbt[ob % 6]
                k.dma("sp", T(w2.ap[:, 0:256].rearrange("p (a b) -> p a b", a=2), w2.res), DR(wple[:, :, ob * 128:(ob + 1) * 128], wk))
                ps2 = k.next_s()
                for kc in range(2):
                    k.mm(ps2, w2[:, kc * 128:(kc + 1) * 128], p_[:, kc * 512:(kc + 1) * 512], start=(kc == 0), stop=(kc == 1))
                k.tt(gs, gs, ps2, ALU.mult)
                k.tt(h8[ob], h8[ob], gs, ALU.add)
            if last:
                rmsnorm(h8, lambda kc: fn_sb[:, kc:kc + 1], tl["h8"][(c + 1) % 2], tl["sq"], rstd)
                for kc in range(8):
                    k.dma("pool", DR(yT[s, kc * 128:(kc + 1) * 128, tok], "out"), tl["h8"][(c + 1) % 2][kc], is_out=True)
            else:
                for kc in range(8):
                    k.dma("pool", DR(hT[s, kc * 128:(kc + 1) * 128, tok], hkey), h8[kc])
        k.release(m0)

    cast_weights(0)
    for l in range(L):
        for s in range(NS):
            if 1 in phases:
                phase1(l, s)
            if s == 0 and l + 1 < L:
                cast_weights(l + 1)
            if 2 in phases:
                phase2_impl(l, s)
            if 3 in phases:
                phase3(l, s, last=(l == L - 1))
    print("ops per engine:", {e: len(q) for e, q in k.s.q.items()}, flush=True)
    k.s.emit(nc, es)


def prep_common(inp, L):
    f = lambda a: np.ascontiguousarray(np.asarray(a, np.float32))
    ng = f(inp["norm_g"])[:L].reshape(L, 4, 8, 128).transpose(3, 0, 1, 2).reshape(128, L * 4 * 8)
    fng = f(inp["final_norm"]).reshape(8, 128).T
    dlam = np.broadcast_to(f(inp["diff_lambda"])[:L].reshape(1, L * 4 * 64), (128, L * 4 * 64))
    subln = f(inp["diff_subln"])[:L].T
    pos = f(inp["nsa_cmp_pos"])[:L]
    posT = pos.transpose(3, 0, 1, 2).reshape(64, L * 2 * 32)
    posT = np.concatenate([posT, posT], axis=0)
    sinks = np.broadcast_to(f(inp["swa_sinks"])[:L].reshape(1, L * 8), (128, L * 8))
    rb = f(inp["rel_bias"])
    rb31 = np.broadcast_to(rb[31:32, :], (128, 20))
    st_a, st_bs, st_bw, st_c, cmp_t = host_tables(rb)
    ov, expand, keep, add, ident = host_consts()
    c = {
        "ng": ng, "fng": fng, "dlam": dlam, "subln": subln, "posT": posT, "sinks": sinks, "rb31": rb31,
        "st_a": st_a, "st_bs": st_bs, "st_bw": st_bw, "st_c": st_c, "cmp_t": cmp_t,
        "ov": ov, "expand": expand, "keep": keep, "addc": add, "ident": ident,
        "ffn1_wi": f(inp["ffn1_wi"])[:L], "ffn2_wi": f(inp["ffn2_wi"])[:L],
        "ffn1_wo": f(inp["ffn1_wo"])[:L], "ffn2_wo": f(inp["ffn2_wo"])[:L],
        "w_in": f(inp["w_in"])[:L], "cmp_w1": f(inp["nsa_cmp_w1"])[:L], "cmp_w2": f(inp["nsa_cmp_w2"])[:L],
        "w_branch": f(inp["w_branch"])[:L], "w_out": f(inp["w_out"])[:L], "w_ple": f(inp["w_ple"])[:L],
        "w_ple_gate": f(inp["w_ple_gate"])[:L],
    }
    return {k_: np.ascontiguousarray(v) for k_, v in c.items()}


def prep_core(inp, seqs, L):
    x = np.asarray(inp["x"], np.float32)
    p = np.asarray(inp["p"], np.float32)
    xT = np.ascontiguousarray(x[seqs].transpose(0, 2, 1))
    pT = np.ascontiguousarray(p[:L][:, seqs].transpose(0, 1, 3, 2))
    return {"xT": xT, "pT": pT}


_PROG = {}


def kernel(**inputs):
    NCORES, NS, L = 8, 2, 4
    key = (NS, L)
    if key not in _PROG:
        _PROG[key] = build_program(NS, L)
    nc = _PROG[key]
    common = prep_common(inputs, L)
    in_maps = []
    for ci in range(NCORES):
        m = dict(common)
        m.update(prep_core(inputs, list(range(ci * NS, (ci + 1) * NS)), L))
        in_maps.append(m)
    res = run_bass_kernel_spmd(nc, in_maps, core_ids=list(range(NCORES)))
    out = np.empty((NCORES * NS, S, D), np.float32)
    for ci in range(NCORES):
        yT = res.results[ci]["yT"]
        for j in range(NS):
            out[ci * NS + j] = yT[j].T
    return out
```
